# Optimizing a Trainium2 kernel written in Bass

```python
import math
import jax, jax.numpy as jnp
from jax import lax
import numpy as np

D_MODEL = 1024
BATCH = 8
SEQ = 4096
DEPTH = 4

GRID_W = 64
CTX_LEN = 256
N_AB = (DEPTH + 1) // 2
N_C = DEPTH // 2
A_HEADS = 8
A_KV_HEADS = 2
A_HEAD_DIM = 64
A_WIDTH = A_HEADS * A_HEAD_DIM
A_KV_WIDTH = A_KV_HEADS * A_HEAD_DIM
WINDOW = 128
Q_BLOCK = WINDOW
ROPE_BASE = 10000.0
B_WIDTH = D_MODEL // 2
CONV_W = 3
AB_IN_WIDTH = 2 * A_WIDTH + 2 * A_KV_WIDTH + 4 * B_WIDTH
AB_OUT_WIDTH = A_WIDTH + B_WIDTH
C_HEADS = 8
C_KEY_DIM = 128
C_VAL_DIM = D_MODEL // C_HEADS
C_F_WIDTH = C_HEADS * C_KEY_DIM
C_I_WIDTH = C_HEADS * C_VAL_DIM
C_IN_WIDTH = 3 * C_F_WIDTH + 2 * C_I_WIDTH
CHUNK = 32
DEEPNORM_ALPHA = (2 * DEPTH) ** 0.25
DEEPNORM_BETA = (8 * DEPTH) ** -0.25
LN_EPS = 1e-5
RMS_EPS = 1e-6

kernel_name = "hybrid_swa_shortconv_hgrn2_dit_block"


def split_cols(p, widths):
    idx = [int(v) for v in np.cumsum(widths)[:-1]]
    return jnp.split(p, idx, axis=-1)


def layer_norm(x, g, b):
    xf = x.astype(jnp.float32)
    mu = xf.mean(-1, keepdims=True)
    var = jnp.square(xf - mu).mean(-1, keepdims=True)
    return ((xf - mu) * lax.rsqrt(var + LN_EPS) * g + b).astype(x.dtype)


def modulate(h, shift, scale):
    return h * (1 + scale) + shift


def rope_1d(x, pos):
    half = x.shape[-1] // 2
    freqs = ROPE_BASE ** (-jnp.arange(half, dtype=jnp.float32) / half)
    ang = pos.astype(jnp.float32)[:, None] * freqs[None, :]
    cos = jnp.cos(ang)[:, None, :].astype(x.dtype)
    sin = jnp.sin(ang)[:, None, :].astype(x.dtype)
    x1, x2 = x[..., :half], x[..., half:]
    return jnp.concatenate([x1 * cos - x2 * sin, x1 * sin + x2 * cos], axis=-1)


def axial_rope(x, row, col):
    half = x.shape[-1] // 2
    return jnp.concatenate([rope_1d(x[..., :half], row), rope_1d(x[..., half:], col)], axis=-1)


def heads(t, n_heads):
    return t.reshape(t.shape[0], t.shape[1], n_heads, A_HEAD_DIM)


def short_conv(u, w):
    pad = CONV_W // 2
    L = u.shape[1]
    up = jnp.pad(u, ((0, 0), (pad, pad), (0, 0)))
    y = up[:, 0:L] * w[0]
    for j in range(1, CONV_W):
        y = y + up[:, j:j + L] * w[j]
    return y


def window_attention(q, k, v, kc, vc, sink):
    B, L = q.shape[:2]
    n_blk = L // Q_BLOCK
    G = A_HEADS // A_KV_HEADS
    scale = A_HEAD_DIM ** -0.5
    lc = kc.shape[1]
    span = Q_BLOCK + 2 * WINDOW
    qb = q.reshape(B, n_blk, Q_BLOCK, A_KV_HEADS, G, A_HEAD_DIM).swapaxes(0, 1)
    kp = jnp.pad(k, ((0, 0), (WINDOW, WINDOW), (0, 0), (0, 0)))
    vp = jnp.pad(v, ((0, 0), (WINDOW, WINDOW), (0, 0), (0, 0)))
    rel = jnp.arange(span)[None, :] - WINDOW - jnp.arange(Q_BLOCK)[:, None]
    in_win = jnp.abs(rel) <= WINDOW
    sink_l = jnp.broadcast_to(sink.astype(jnp.float32).reshape(1, A_KV_HEADS, G, 1, 1),
                              (B, A_KV_HEADS, G, Q_BLOCK, 1))

    def one_block(args):
        n, q_n = args
        start = n * Q_BLOCK
        k_n = lax.dynamic_slice_in_dim(kp, start, span, axis=1)
        v_n = lax.dynamic_slice_in_dim(vp, start, span, axis=1)
        key_pos = start - WINDOW + jnp.arange(span)
        valid = in_win & ((key_pos >= 0) & (key_pos < L))[None, :]
        s_loc = jnp.einsum('bikgd,bjkd->bkgij', q_n, k_n).astype(jnp.float32) * scale
        s_loc = jnp.where(valid, s_loc, -jnp.inf)
        s_ctx = jnp.einsum('bikgd,bjkd->bkgij', q_n, kc).astype(jnp.float32) * scale
        p = jax.nn.softmax(jnp.concatenate([sink_l, s_ctx, s_loc], axis=-1), axis=-1).astype(v.dtype)
        return (jnp.einsum('bkgij,bjkd->bikgd', p[..., 1:1 + lc], vc)
                + jnp.einsum('bkgij,bjkd->bikgd', p[..., 1 + lc:], v_n))

    out = lax.map(one_block, (jnp.arange(n_blk), qb))
    return out.swapaxes(0, 1).reshape(B, L, A_WIDTH)


def context_attention(qc, kc, vc, sink):
    B, lc = qc.shape[:2]
    G = A_HEADS // A_KV_HEADS
    q = qc.reshape(B, lc, A_KV_HEADS, G, A_HEAD_DIM)
    s = jnp.einsum('bikgd,bjkd->bkgij', q, kc).astype(jnp.float32) * (A_HEAD_DIM ** -0.5)
    sink_l = jnp.broadcast_to(sink.astype(jnp.float32).reshape(1, A_KV_HEADS, G, 1, 1),
                              (B, A_KV_HEADS, G, lc, 1))
    p = jax.nn.softmax(jnp.concatenate([sink_l, s], axis=-1), axis=-1).astype(vc.dtype)
    return jnp.einsum('bkgij,bjkd->bikgd', p[..., 1:], vc).reshape(B, lc, A_WIDTH)


def ab_layer(x, xc, mod, mod_c, w_in, w_out, sink, conv_w, ln_g, ln_b, row, col, ctx_out):
    shift, scale, gate = jnp.split(mod, 3, axis=-1)
    shift_c, scale_c, gate_c = jnp.split(mod_c, 3, axis=-1)
    widths = (A_WIDTH, A_KV_WIDTH, A_KV_WIDTH, A_WIDTH, B_WIDTH, B_WIDTH, B_WIDTH, B_WIDTH)
    q, k, v, g_a, xb, b_g, c_g, g_b = split_cols(modulate(x, shift, scale) @ w_in, widths)
    hc = modulate(xc, shift_c, scale_c)
    if ctx_out:
        qc, kc, vc, g_ac, xbc, b_gc, c_gc, g_bc = split_cols(hc @ w_in, widths)
    else:
        kc, vc = split_cols(hc @ w_in[:, A_WIDTH:A_WIDTH + 2 * A_KV_WIDTH], (A_KV_WIDTH, A_KV_WIDTH))
    kc, vc = heads(kc, A_KV_HEADS), heads(vc, A_KV_HEADS)
    q = axial_rope(heads(q, A_HEADS), row, col)
    k = axial_rope(heads(k, A_KV_HEADS), row, col)
    o_a = window_attention(q, k, heads(v, A_KV_HEADS), kc, vc, sink)
    o_b = b_g * short_conv(c_g * xb, conv_w)
    y = jnp.concatenate([o_a * jax.nn.silu(g_a), o_b * jax.nn.silu(g_b)], axis=-1) @ w_out
    x_new = layer_norm(DEEPNORM_ALPHA * x + gate * y, ln_g, ln_b)
    if not ctx_out:
        return x_new, None
    o_ac = context_attention(heads(qc, A_HEADS), kc, vc, sink)
    o_bc = b_gc * short_conv(c_gc * xbc, conv_w)
    yc = jnp.concatenate([o_ac * jax.nn.silu(g_ac), o_bc * jax.nn.silu(g_bc)], axis=-1) @ w_out
    xc_new = layer_norm(DEEPNORM_ALPHA * xc + gate_c * yc, ln_g, ln_b)
    return x_new, xc_new


def to_heads(t, d):
    B, L, _ = t.shape
    return t.reshape(B, L, -1, d).transpose(0, 2, 1, 3).astype(jnp.float32)


def forget_terms(z, lb):
    lb = jnp.clip(lb.astype(jnp.float32), 0.0, 1.0).reshape(C_HEADS, 1, C_KEY_DIM)
    log_f = jnp.logaddexp(jnp.log(lb), jnp.log1p(-lb) + jax.nn.log_sigmoid(z))
    k = (1 - lb) * jax.nn.sigmoid(-z)
    return k, log_f


def hgrn2_scan(q, k, v, log_f, s0, with_outputs):
    B, H, L, dk = k.shape
    n = L // CHUNK
    chunks = lambda t: t.reshape(B, H, n, CHUNK, t.shape[-1])
    to_scan = lambda t: jnp.moveaxis(t, 2, 0)
    k, v, log_f = chunks(k), chunks(v), chunks(log_f)
    b = jnp.cumsum(log_f, axis=3)
    b_last = b[:, :, :, -1:, :]
    k_state = k * jnp.exp(b_last - b)
    decay = jnp.exp(b_last[:, :, :, 0, :])
    if with_outputs:
        q = chunks(q)
        q_inter = q * jnp.exp(b)
        tri = jnp.tril(jnp.ones((CHUNK, CHUNK), dtype=bool))[:, :, None]

        def step(s, xs):
            q_n, k_n, v_n, b_n, qi_n, ks_n, d_n = xs
            o_inter = jnp.einsum('bhcd,bhde->bhce', qi_n, s)
            diff = b_n[:, :, :, None, :] - b_n[:, :, None, :, :]
            w = jnp.exp(jnp.where(tri, diff, -jnp.inf))
            scores = jnp.einsum('bhcd,bhsd,bhcsd->bhcs', q_n, k_n, w)
            o_n = o_inter + jnp.einsum('bhcs,bhse->bhce', scores, v_n)
            return d_n[..., None] * s + jnp.einsum('bhcd,bhce->bhde', ks_n, v_n), o_n

        s_final, o = lax.scan(step, s0, (to_scan(q), to_scan(k), to_scan(v), to_scan(b),
                                         to_scan(q_inter), to_scan(k_state), to_scan(decay)))
        o = jnp.moveaxis(o, 0, 2).reshape(B, H, L, v.shape[-1])
        return o, s_final

    def step_state(s, xs):
        k_n, v_n, d_n = xs
        return d_n[..., None] * s + jnp.einsum('bhcd,bhce->bhde', k_n, v_n), None

    s_final, _ = lax.scan(step_state, s0, (to_scan(k_state), to_scan(v), to_scan(decay)))
    return None, s_final


def hgrn2_direction(q, i, z, lb, qc, ic, zc, ctx_out, reverse):
    flip = (lambda t: jnp.flip(t, axis=2)) if reverse else (lambda t: t)
    B = i.shape[0]
    kc, lfc = forget_terms(zc, lb)
    s0 = jnp.zeros((B, C_HEADS, C_KEY_DIM, C_VAL_DIM), jnp.float32)
    oc, s_ctx = hgrn2_scan(flip(qc) if ctx_out else None, flip(kc), flip(ic), flip(lfc), s0, ctx_out)
    k, lf = forget_terms(z, lb)
    o, _ = hgrn2_scan(flip(q), flip(k), flip(i), flip(lf), s_ctx, True)
    return flip(o), (flip(oc) if ctx_out else None)


def hgrn2_readout(o, g, g_norm, w_out):
    on = o * lax.rsqrt(jnp.mean(o * o, axis=-1, keepdims=True) + RMS_EPS) * g_norm
    B, H, L, dv = on.shape
    return (on.transpose(0, 2, 1, 3).reshape(B, L, H * dv).astype(g.dtype) * jax.nn.silu(g)) @ w_out


def c_layer(x, xc, mod, mod_c, w_in, w_out, lb_f, lb_b, g_norm, ln_g, ln_b, ctx_out):
    shift, scale, gate = jnp.split(mod, 3, axis=-1)
    shift_c, scale_c, gate_c = jnp.split(mod_c, 3, axis=-1)
    widths = (C_F_WIDTH, C_F_WIDTH, C_F_WIDTH, C_I_WIDTH, C_I_WIDTH)
    q, zf, zb, i, g = split_cols(modulate(x, shift, scale) @ w_in, widths)
    hc = modulate(xc, shift_c, scale_c)
    if ctx_out:
        qc, zfc, zbc, ic, gc = split_cols(hc @ w_in, widths)
        qc = jax.nn.silu(to_heads(qc, C_KEY_DIM)) * (C_KEY_DIM ** -0.5)
    else:
        zfc, zbc, ic = split_cols(hc @ w_in[:, C_F_WIDTH:3 * C_F_WIDTH + C_I_WIDTH],
                                  (C_F_WIDTH, C_F_WIDTH, C_I_WIDTH))
        qc = None
    q = jax.nn.silu(to_heads(q, C_KEY_DIM)) * (C_KEY_DIM ** -0.5)
    i, zf, zb = to_heads(i, C_VAL_DIM), to_heads(zf, C_KEY_DIM), to_heads(zb, C_KEY_DIM)
    ic, zfc, zbc = to_heads(ic, C_VAL_DIM), to_heads(zfc, C_KEY_DIM), to_heads(zbc, C_KEY_DIM)
    o_f, oc_f = hgrn2_direction(q, i, zf, lb_f, qc, ic, zfc, ctx_out, reverse=False)
    o_b, oc_b = hgrn2_direction(q, i, zb, lb_b, qc, ic, zbc, ctx_out, reverse=True)
    y = hgrn2_readout(o_f + o_b, g, g_norm, w_out)
    x_new = layer_norm(DEEPNORM_ALPHA * x + gate * y, ln_g, ln_b)
    if not ctx_out:
        return x_new, None
    yc = hgrn2_readout(oc_f + oc_b, gc, g_norm, w_out)
    xc_new = layer_norm(DEEPNORM_ALPHA * xc + gate_c * yc, ln_g, ln_b)
    return x_new, xc_new


def setup_inputs(seed: int = 0) -> dict:
    key = jax.random.key(seed)
    ks = jax.random.split(key, 16)
    nrm = lambda k, shape, s: jax.random.normal(k, shape, jnp.float32) * s
    ab_col_scale = jnp.concatenate([
        jnp.ones((A_WIDTH + A_KV_WIDTH,), jnp.float32),
        jnp.full((A_KV_WIDTH,), DEEPNORM_BETA, jnp.float32),
        jnp.ones((A_WIDTH + 4 * B_WIDTH,), jnp.float32)])
    c_col_scale = jnp.concatenate([
        jnp.ones((3 * C_F_WIDTH,), jnp.float32),
        jnp.full((C_I_WIDTH,), DEEPNORM_BETA, jnp.float32),
        jnp.ones((C_I_WIDTH,), jnp.float32)])
    return {
        "x": nrm(ks[0], (BATCH, SEQ, D_MODEL), 1.0),
        "c": nrm(ks[1], (BATCH, D_MODEL), 1.0),
        "ctx": nrm(ks[2], (BATCH, CTX_LEN, D_MODEL), 1.0),
        "c_ctx": nrm(ks[3], (D_MODEL,), 1.0),
        "w_ada": nrm(ks[4], (DEPTH, D_MODEL, 3 * D_MODEL), D_MODEL ** -0.5),
        "b_ada": nrm(ks[5], (DEPTH, 3 * D_MODEL), 0.02),
        "ln_g": 1.0 + nrm(ks[6], (DEPTH, D_MODEL), 0.02),
        "ln_b": nrm(ks[7], (DEPTH, D_MODEL), 0.02),
        "w_in_ab": nrm(ks[8], (N_AB, D_MODEL, AB_IN_WIDTH), D_MODEL ** -0.5) * ab_col_scale,
        "w_out_ab": nrm(ks[9], (N_AB, AB_OUT_WIDTH, D_MODEL), AB_OUT_WIDTH ** -0.5 * DEEPNORM_BETA),
        "sink_ab": nrm(ks[10], (N_AB, A_HEADS), 0.5),
        "conv_ab": nrm(ks[11], (N_AB, CONV_W, B_WIDTH), CONV_W ** -0.5),
        "w_in_c": nrm(ks[12], (N_C, D_MODEL, C_IN_WIDTH), D_MODEL ** -0.5) * c_col_scale,
        "w_out_c": nrm(ks[13], (N_C, C_I_WIDTH, D_MODEL), C_I_WIDTH ** -0.5 * DEEPNORM_BETA),
        "lb_c": nrm(ks[14], (2, N_C, C_F_WIDTH), 0.5),
        "gnorm_c": 1.0 + nrm(ks[15], (N_C, C_VAL_DIM), 0.02),
    }


def reference(x, c, ctx, c_ctx, w_ada, b_ada, ln_g, ln_b, w_in_ab, w_out_ab, sink_ab, conv_ab,
              w_in_c, w_out_c, lb_c, gnorm_c):
    L = x.shape[1]
    rows = L // GRID_W
    row = jnp.repeat(jnp.arange(rows), GRID_W)
    col = jnp.tile(jnp.arange(GRID_W), rows)
    lb_p = jax.nn.softmax(lb_c.astype(jnp.float32), axis=1)
    lb_all = jnp.cumsum(lb_p, axis=1) - lb_p[:, :1]
    silu_c = jax.nn.silu(c)
    silu_cc = jax.nn.silu(c_ctx)
    xc = ctx
    for l in range(DEPTH):
        mod = (silu_c @ w_ada[l] + b_ada[l])[:, None, :]
        mod_c = silu_cc @ w_ada[l] + b_ada[l]
        ctx_out = l < DEPTH - 1
        j = l // 2
        if l % 2 == 0:
            x, xc = ab_layer(x, xc, mod, mod_c, w_in_ab[j], w_out_ab[j], sink_ab[j], conv_ab[j],
                             ln_g[l], ln_b[l], row, col, ctx_out)
        else:
            x, xc = c_layer(x, xc, mod, mod_c, w_in_c[j], w_out_c[j], lb_all[0, j], lb_all[1, j],
                            gnorm_c[j], ln_g[l], ln_b[l], ctx_out)
    return x
```

```python
import math
from contextlib import ExitStack
import numpy as np
import concourse.bass as bass
import concourse.mybir as mybir
from concourse.bass_utils import run_bass_kernel_spmd

AF = mybir.ActivationFunctionType
ALU = mybir.AluOpType
F32 = mybir.dt.float32
BF16 = mybir.dt.bfloat16
AX = mybir.AxisListType

D = 1024
CTX = 256
DEPTH = 4
ALPHA = (2 * DEPTH) ** 0.25
LN_EPS = 1e-5
RMS_EPS = 1e-6
AB_W = 3968
C_W = 5120

QUEUES = ['pe', 'act', 'dve', 'pool', 'sp']
SAME_ENG_SYNC = {'pe': False, 'act': True, 'dve': True, 'pool': True, 'sp': False}


class Tk:
    __slots__ = ('name', 'w', 'r', 'sem', 'semv')

    def __init__(self, name):
        self.name = name
        self.w = None
        self.r = []
        self.sem = None
        self.semv = 0


class Prog:
    def __init__(self, nc):
        self.nc = nc
        self.ops = {q: [] for q in QUEUES}

    def op(self, q, fn, reads=(), writes=(), dma=None):
        deps = []
        for t in reads:
            if t.w is not None:
                deps.append(t.w)
        for t in writes:
            if t.w is not None:
                deps.append(t.w)
            deps.extend(t.r)
        idx = len(self.ops[q])
        if dma is not None:
            if dma.sem is None:
                dma.sem = self.nc.alloc_semaphore("d_" + dma.name)
            dma.semv += 16
            tok = ('dma', dma, dma.semv)
        else:
            tok = ('eng', q, idx)
        self.ops[q].append(dict(fn=fn, deps=deps, tok=tok))
        for t in reads:
            if tok[0] == 'eng':
                t.r = [x for x in t.r if not (x[0] == 'eng' and x[1] == q)]
            t.r.append(tok)
        for t in writes:
            t.w = tok
            t.r = []
        return tok

    def emit(self):
        nc = self.nc
        sems = {q: nc.alloc_semaphore("q_" + q) for q in QUEUES}
        needed = {q: set() for q in QUEUES}
        for q in QUEUES:
            waited = {}
            for idx, o in enumerate(self.ops[q]):
                req = {}
                for d in o['deps']:
                    if d[0] == 'eng':
                        if d[1] == q and not SAME_ENG_SYNC[q]:
                            continue
                        key = ('eng', d[1])
                    else:
                        key = ('dma', id(d[1]), d[1])
                    if d[2] > req.get(key, -1):
                        req[key] = d[2]
                waits = []
                for key, val in req.items():
                    if waited.get(key, -1) >= val:
                        continue
                    waited[key] = val
                    waits.append((key, val))
                    if key[0] == 'eng':
                        needed[key[1]].add(val)
                o['waits'] = waits
        ms = {}
        for q in QUEUES:
            ms[q] = {idx: rank + 1 for rank, idx in enumerate(sorted(needed[q]))}
        with nc.Block() as block:
            def body(q):
                def f(e):
                    for idx, o in enumerate(self.ops[q]):
                        for key, val in o['waits']:
                            if key[0] == 'eng':
                                e.wait_ge(sems[key[1]], ms[key[1]][val])
                            else:
                                e.wait_ge(key[2].sem, val)
                        ins = o['fn'](e)
                        tok = o['tok']
                        if tok[0] == 'dma':
                            ins.then_inc(tok[1].sem, 16)
                        elif idx in ms[q]:
                            ins.then_inc(sems[q], 1)
                return f
            block.tensor(body('pe'))
            block.scalar(body('act'))
            block.vector(body('dve'))
            block.gpsimd(body('pool'))
            block.sync(body('sp'))


def interleave(*gens):
    gens = [g for g in gens if g is not None]
    while gens:
        for g in list(gens):
            try:
                next(g)
            except StopIteration:
                gens.remove(g)


def run_gen(g):
    for _ in g:
        pass


class Buf:
    def __init__(self, t, name):
        self.t = t
        self.k = Tk(name)

    def __getitem__(self, key):
        return self.t[key]


def build_program(L=4096, layers=(0, 1, 2, 3), dbg=False):
    NT = L // 128
    NTT = NT + 2
    nc = bass.Bass("TRN2", target_bir_lowering=False)
    P = Prog(nc)
    _n = [0]

    cur_stack = [None]

    def sb(shape, dtype, name=None):
        _n[0] += 1
        name = (name or "b") + "_%d" % _n[0]
        if cur_stack[0] is not None:
            return Buf(cur_stack[0].enter_context(nc.sbuf_tensor(name, list(shape), dtype)), name)
        return Buf(nc.alloc_sbuf_tensor(name, list(shape), dtype), name)

    def din(name, shape, dtype=F32):
        return nc.dram_tensor(name, list(shape), dtype, kind="ExternalInput").ap()

    def dscr(name, shape, dtype=F32):
        return nc.dram_tensor(name, list(shape), dtype).ap()

    x_in = din("x", [L, D])
    ctx_in = din("ctx", [CTX, D])
    ccols_in = din("ccols", [128, 8, 2])
    w_ada = din("w_ada", [DEPTH, D, 3 * D])
    bcols_in = din("bcols", [DEPTH, 128, 24])
    ln_g = din("ln_g", [DEPTH, 1, D])
    ln_b = din("ln_b", [DEPTH, 1, D])
    w_in_ab = din("w_in_ab", [2, D, AB_W])
    w_out_ab = din("w_out_ab", [2, D, D])
    sink_ab = din("sink_ab", [2, 1, 8])
    convc = din("convc", [2, 128, 4, 3])
    w_in_c = din("w_in_c", [2, D, C_W])
    w_out_c = din("w_out_c", [2, D, D])
    lbcols = din("lbcols", [2, 128, 8, 2])
    gnorm_c = din("gnorm_c", [2, 1, 128])
    cos_in = din("cos_t", [128, L])
    sin_in = din("sin_t", [128, L])
    out_d = nc.dram_tensor("out", [L, D], F32, kind="ExternalOutput").ap()
    xs = [dscr("xs0", [NTT * 128, D]), dscr("xs1", [NTT * 128, D])]
    sc_q = dscr("sc_q", [NTT, 128, 1024], BF16)
    sc_k = dscr("sc_k", [NTT, 128, 1024], BF16)
    sc_v = dscr("sc_v", [NTT, 128, 1024], BF16)
    sc_of = dscr("sc_of", [NTT, 128, 1024], F32)
    sc_sg = dscr("sc_sg", [NTT, 128, 1024], F32)
    xsk = [[Tk("xs%d_%d" % (s_, i)) for i in range(NTT)] for s_ in range(2)]
    sck = [Tk("sc_%d" % i) for i in range(NTT)]
    out_stores = []

    def ks(bufs):
        return [b.k if isinstance(b, Buf) else b for b in bufs]

    def MM(out, lhsT, rhs, start=True, stop=True, R=(), W=(), tp=None, sgc=False):
        kw = {}
        if tp is not None:
            kw['tile_position'] = tp
        if sgc:
            kw['skip_group_check'] = True
        P.op('pe', lambda e: e.matmul(out=out, lhsT=lhsT, rhs=rhs, start=start, stop=stop, **kw), ks(R), ks(W))

    def TR(out, in_, ident, R=(), W=()):
        P.op('pe', lambda e: e.transpose(out=out, in_=in_, identity=ident), ks(R), ks(W))

    def ACT(out, in_, func, R=(), W=(), scale=None, bias=None):
        kw = {}
        if scale is not None:
            kw['scale'] = scale
        if bias is not None:
            kw['bias'] = bias
        P.op('act', lambda e: e.activation(out=out, in_=in_, func=func, **kw), ks(R), ks(W))

    def TT(q, out, in0, in1, op, R=(), W=()):
        P.op(q, lambda e: e.tensor_tensor(out=out, in0=in0, in1=in1, op=op), ks(R), ks(W))

    def TS(q, out, in0, s1, s2, op0, op1, R=(), W=()):
        P.op(q, lambda e: e.tensor_scalar(out=out, in0=in0, scalar1=s1, scalar2=s2, op0=op0, op1=op1), ks(R), ks(W))

    def STT(out, in0, scalar, in1, op0, op1, R=(), W=()):
        P.op('dve', lambda e: e.scalar_tensor_tensor(out=out, in0=in0, scalar=scalar, in1=in1, op0=op0, op1=op1),
             ks(R), ks(W))

    def CP(q, out, in_, R=(), W=()):
        if q == 'act':
            P.op('act', lambda e: e.copy(out=out, in_=in_), ks(R), ks(W))
        else:
            P.op(q, lambda e: e.tensor_copy(out=out, in_=in_), ks(R), ks(W))

    def MEMSET(q, ap, val, W=()):
        P.op(q, lambda e: e.memset(ap, val), (), ks(W))

    all_dma_tk = []

    def DMA(q, out, in_, R=(), W=(), tk=None, mld=None):
        tkk = tk.k if isinstance(tk, Buf) else tk
        if tkk not in all_dma_tk:
            all_dma_tk.append(tkk)
        kw = {}
        if mld is not None:
            kw['max_dma_last_dim'] = mld
        return P.op(q, lambda e: e.dma_start(out=out, in_=in_, **kw), ks(R), ks(W), dma=tkk)

    def barrier():
        deps = []
        for q in ['pe', 'act', 'dve', 'pool']:
            if P.ops[q]:
                deps.append(('eng', q, len(P.ops[q]) - 1))
        for t in all_dma_tk:
            if t.sem is not None:
                deps.append(('dma', t, t.semv))
        for q in QUEUES:
            P.ops[q].append(dict(fn=lambda e: e.nop(), deps=list(deps), tok=('eng', q, len(P.ops[q]))))

    ident = sb([128, 128], F32, "ident")
    identb = sb([128, 128], BF16, "identb")
    ones_f = sb([128, 128], F32, "ones_f")
    ones_b = sb([128, 512], BF16, "ones_b")
    MEMSET('pool', ident[:], 0.0, W=[ident])
    P.op('pool', lambda e: e.affine_select(out=ident[:], in_=ident[:], pattern=[[-1, 128]], compare_op=ALU.not_equal,
                                           fill=1.0, base=0, channel_multiplier=1), ks([ident]), ks([ident]))
    CP('pool', identb[:], ident[:], R=[ident], W=[identb])
    MEMSET('pool', ones_f[:], 1.0, W=[ones_f])
    MEMSET('pool', ones_b[:], 1.0, W=[ones_b])
    eps_ln = sb([128, 1], F32, "eps_ln")
    MEMSET('pool', eps_ln[:], LN_EPS, W=[eps_ln])
    one_col = sb([128, 1], F32, "one_col")
    MEMSET('pool', one_col[:], 1.0, W=[one_col])
    mhalf = sb([128, 8], F32, "mhalf")
    MEMSET('pool', mhalf[:], -0.5, W=[mhalf])

    WB = sb([128, 8, C_W], BF16, "WB")
    WBk = [Tk("WBk%d" % i) for i in range(8)]
    WOB = sb([128, 8, D], BF16, "WOB")
    WOBk = [Tk("WOBk%d" % i) for i in range(8)]
    PS = nc.alloc_psum_tensor("PS", [128, 8, 512], F32)
    PSk = [Tk("ps%d" % i) for i in range(8)]
    ring_ctr = {'A': 0, 'B': 0, 'C': 0}

    def ps_ring(role):
        base = {'A': 0, 'B': 2, 'C': 4}[role]
        i = base + ring_ctr[role] % 2
        ring_ctr[role] += 1
        return i

    ccols = sb([128, 8, 2], F32, "ccols")
    scols = sb([128, 8, 2], F32, "scols")
    DMA('sp', ccols[:], ccols_in, W=[ccols], tk=ccols)
    ACT(scols[:], ccols[:], AF.Silu, R=[ccols], W=[scols])
    bcol = sb([128, 24], F32, "bcol")
    modall = sb([128, 24, 2], F32, "modall")
    modc = modall
    gate_rep = [sb([128, D], F32, "gate%d" % i) for i in range(2)]
    lng_rep = sb([128, D], F32, "lng")
    lnb_rep = sb([128, D], F32, "lnb")
    dg_ring = [sb([128, 128], F32, "dg%d" % i) for i in range(2)]
    ep_ring = [sb([128, D], F32, "ep%d" % i) for i in range(2)]
    wada_ctr = [0]

    def layer_setup(l):
        if l % 2 == 0:
            wi, wo, Wd = w_in_ab[l // 2], w_out_ab[l // 2], AB_W
        else:
            wi, wo, Wd = w_in_c[l // 2], w_out_c[l // 2], C_W
        for kc in range(8):
            DMA('pool', WB[:, kc, 0:Wd], wi[kc * 128:(kc + 1) * 128, :], W=[WBk[kc]], tk=WBk[kc], mld=4096)
        for kc in range(8):
            DMA('pool', WOB[:, kc, :], wo[kc * 128:(kc + 1) * 128, :], W=[WOBk[kc]], tk=WOBk[kc], mld=4096)
        DMA('sp', bcol[:], bcols_in[l], W=[bcol], tk=bcol)
        DMA('sp', lng_rep[:], ln_g[l].partition_broadcast(128), W=[lng_rep], tk=lng_rep)
        DMA('sp', lnb_rep[:], ln_b[l].partition_broadcast(128), W=[lnb_rep], tk=lnb_rep)
        for cc in range(24):
            wab = ep_ring[wada_ctr[0] % 2]
            wada_ctr[0] += 1
            wa = wab[:, :].rearrange("p (kc n) -> p kc n", kc=8)
            DMA('sp', wa, w_ada[l][:, cc * 128:(cc + 1) * 128].rearrange("(kc p) n -> p kc n", p=128), W=[wab], tk=wab)
            o = PS[:, 0, cc * 2:cc * 2 + 2]
            for kc in range(8):
                MM(o, wa[:, kc, :], scols[:, kc, :], start=(kc == 0), stop=(kc == 7), R=[wab, scols], W=[PSk[0]])
        TT('dve', modall[:], PS[:, 0, 0:48].rearrange("p (c w) -> p c w", w=2),
           bcol[:].unsqueeze(2).to_broadcast([128, 24, 2]), ALU.add, R=[PSk[0], bcol], W=[modall])
        TS('dve', modall[:, 8:16, :], modall[:, 8:16, :], 1.0, None, ALU.add, ALU.bypass, R=[modall], W=[modall])
        dgc = 0
        for w in range(2):
            for half in range(2):
                for j_ in range(4):
                    kc = half * 4 + j_
                    dg = dg_ring[dgc % 2]
                    dgc += 1
                    TS('dve', dg[:], ident[:], modall[:, 16 + kc, w:w + 1], None, ALU.mult, ALU.bypass,
                       R=[ident, modall], W=[dg])
                    MM(PS[:, 1, j_ * 128:(j_ + 1) * 128], ones_f[:], dg[:], R=[ones_f, dg], W=[PSk[1]])
                CP('dve', gate_rep[w][:, half * 512:(half + 1) * 512], PS[:, 1, :], R=[PSk[1]], W=[gate_rep[w]])

    def src_ap(l, ti):
        if l == layers[0]:
            if ti < 2:
                return ctx_in[ti * 128:(ti + 1) * 128, :], None
            return x_in[(ti - 2) * 128:(ti - 1) * 128, :], None
        s = (layers.index(l) + 1) % 2
        return xs[s][ti * 128:(ti + 1) * 128, :], xsk[s][ti]

    def dst_ap(l, ti):
        if l == layers[-1]:
            if ti < 2:
                return None, None
            return out_d[(ti - 2) * 128:(ti - 1) * 128, :], None
        s = (layers.index(l)) % 2
        return xs[s][ti * 128:(ti + 1) * 128, :], xsk[s][ti]

    def load_x(l, ti, xt):
        ap, dk = src_ap(l, ti)
        DMA('sp', xt[:], ap, R=[dk] if dk else [], W=[xt], tk=xt)

    def transpose_mod(xt, hT, w, on_dve=False):
        for half in range(2):
            b = ps_ring('A')
            for j in range(4):
                kc = half * 4 + j
                TR(PS[:, b, j * 128:(j + 1) * 128], xt[:, kc * 128:(kc + 1) * 128], ident[:], R=[xt, ident], W=[PSk[b]])
            for j in range(4):
                kc = half * 4 + j
                if on_dve:
                    TS('dve', hT[:, kc, :], PS[:, b, j * 128:(j + 1) * 128], modc[:, 8 + kc, w:w + 1], modc[:, kc, w:w + 1],
                       ALU.mult, ALU.add, R=[PSk[b], modc], W=[hT])
                else:
                    ACT(hT[:, kc, :], PS[:, b, j * 128:(j + 1) * 128], AF.Identity, R=[PSk[b], modc], W=[hT],
                        scale=modc[:, 8 + kc, w:w + 1], bias=modc[:, kc, w:w + 1])

    ep_ctr = [0]
    stats = sb([128, 2, 6], F32, "stats")
    mv = sb([128, 2], F32, "mv")
    rstd = sb([128, 1], F32, "rstd")

    def epilogue(l, ti, ybanks, xt, w):
        dap, dk = dst_ap(l, ti)
        if dap is None:
            return
        r = ep_ring[ep_ctr[0] % 2]
        ep_ctr[0] += 1
        yv = PS[:, ybanks[0]:ybanks[0] + 2, :].rearrange("p a b -> p (a b)")
        TT('dve', r[:], yv, gate_rep[w][:], ALU.mult, R=[PSk[ybanks[0]], PSk[ybanks[1]], gate_rep[w]], W=[r])
        STT(r[:], xt[:], ALPHA, r[:], ALU.mult, ALU.add, R=[xt, r], W=[r])
        P.op('dve', lambda e: e.bn_stats(out=stats[:, 0, :], in_=r[:, 0:512]), ks([r]), ks([stats]))
        P.op('dve', lambda e: e.bn_stats(out=stats[:, 1, :], in_=r[:, 512:1024]), ks([r, stats]), ks([stats]))
        P.op('dve', lambda e: e.bn_aggr(out=mv[:], in_=stats[:].rearrange("p a b -> p (a b)")), ks([stats]), ks([mv]))
        TS('dve', rstd[:], mv[:, 1:2], eps_ln[:, 0:1], None, ALU.add, ALU.bypass, R=[mv, eps_ln], W=[rstd])
        TT('pool', rstd[:], rstd[:], mhalf[:, 0:1], ALU.pow, R=[rstd, mhalf], W=[rstd])
        TS('dve', r[:], r[:], mv[:, 0:1], rstd[:, 0:1], ALU.subtract, ALU.mult, R=[r, mv, rstd], W=[r])
        TT('dve' if l % 2 == 1 else 'pool', r[:], r[:], lng_rep[:], ALU.mult, R=[r, lng_rep], W=[r])
        TT('pool', r[:], r[:], lnb_rep[:], ALU.add, R=[r, lnb_rep], W=[r])
        tok = DMA('sp', dap, r[:], R=[r], W=[dk] if dk else [], tk=r)
        if l == layers[-1]:
            out_stores.append(r)

    def ab_layer(l):
        j = l // 2
        layer_setup(l)
        NXT = 3
        xt_ring = [sb([128, D], F32, "xt%d" % i) for i in range(NXT)]
        hT_ring = [sb([128, 8, 128], BF16, "hT%d" % i) for i in range(2)]
        qf_ring = [sb([128, 4, 128], BF16, "qf%d" % i) for i in range(2)]
        sga_ring = [sb([128, 4, 128], F32, "sga%d" % i) for i in range(2)]
        U_ring = [sb([128, 4, 130], F32, "U%d" % i) for i in range(3)]
        bg_ring = [sb([128, 4, 128], F32, "bg%d" % i) for i in range(2)]
        sgb_ring = [sb([128, 4, 128], F32, "sgb%d" % i) for i in range(2)]
        cs_ring = [sb([128, 2, 128], F32, "cs%d" % i) for i in range(2)]
        def kTs(pr, kt):
            return WB[pr, kt // 9, AB_W + (kt % 9) * 128:AB_W + (kt % 9 + 1) * 128]
        kTk = [Tk("kT%d" % i) for i in range(NTT)]
        Vx = sb([128, NTT, 256], BF16, "Vx")
        Vxk = [Tk("Vx%d" % i) for i in range(NTT)]
        t1 = sb([128, 4, 128], F32, "t1")
        t2 = sb([128, 4, 128], F32, "t2")
        xb_sb = sb([128, 4, 128], F32, "xb_sb")
        PT_ring = [sb([128, 512], BF16, "PT%d" % i) for i in range(3)]
        pt_ctr = [0]
        rec = sb([128, 512], F32, "rec")
        ot = sb([128, 512], F32, "ot")
        aT = sb([128, 4, 128], BF16, "aT")
        bT = sb([128, 4, 128], BF16, "bT")
        cy = sb([128, 4, 128], F32, "cy")
        maskL = sb([128, 4, 128], BF16, "maskL")
        maskR = sb([128, 4, 128], BF16, "maskR")
        cw = sb([128, 4, 3], F32, "cw")
        sk = sb([1, 8], F32, "sk")
        esrow = sb([1, 8, 128], BF16, "esrow")
        sel = sb([1, 2, 128], BF16, "sel")
        CP('pool', maskL[:], ones_b[:, 0:512].rearrange("p (a b) -> p a b", a=4), R=[ones_b], W=[maskL])
        CP('pool', maskR[:], ones_b[:, 0:512].rearrange("p (a b) -> p a b", a=4), R=[ones_b], W=[maskR])
        P.op('pool', lambda e: e.affine_select(out=maskL[:], in_=maskL[:], pattern=[[0, 4], [-1, 128]], compare_op=ALU.is_ge,
                                               fill=0.0, base=0, channel_multiplier=1), ks([maskL]), ks([maskL]))
        P.op('pool', lambda e: e.affine_select(out=maskR[:], in_=maskR[:], pattern=[[0, 4], [1, 128]], compare_op=ALU.is_ge,
                                               fill=0.0, base=0, channel_multiplier=-1), ks([maskR]), ks([maskR]))
        DMA('sp', cw[:], convc[j], W=[cw], tk=cw)
        DMA('sp', sk[:], sink_ab[j], W=[sk], tk=sk)
        ACT(sk[:], sk[:], AF.Exp, R=[sk], W=[sk])
        CP('dve', esrow[:], sk[0:1, :].unsqueeze(2).to_broadcast([1, 8, 128]), R=[sk], W=[esrow])
        MEMSET('pool', sel[:], 0.0, W=[sel])
        MEMSET('pool', sel[0:1, 0, 64:128], 1.0, W=[sel])
        MEMSET('pool', sel[0:1, 1, 0:64], 1.0, W=[sel])
        if dbg:
            print("AB sbuf remaining", nc.sbuf_bytes_remaining)
        MEMSET('pool', Vx[:], 1.0, W=Vxk)
        for u in U_ring:
            MEMSET('pool', u[:], 0.0, W=[u])

        def bankv(b):
            return PS[:, b, :].rearrange("p (a t) -> p a t", a=4)

        def proj_fm(hT, bank, chunks):
            for s_, cc in enumerate(chunks):
                for kc in range(8):
                    MM(PS[:, bank, s_ * 128:(s_ + 1) * 128], WB[:, kc, cc * 128:(cc + 1) * 128], hT[:, kc, :],
                       start=(kc == 0), stop=(kc == 7), R=[WBk[kc], hT], W=[PSk[bank]])

        def stage1a(ti):
            is_ctx = ti < 2
            w = 1 if is_ctx else 0
            xt = xt_ring[ti % NXT]
            hT = hT_ring[ti % 2]
            U = U_ring[ti % 3]
            cs = cs_ring[ti % 2]
            load_x(l, ti, xt)
            if not is_ctx:
                t0 = (ti - 2) * 128
                DMA('sp', cs[:, 0, :], cos_in[:, t0:t0 + 128], W=[cs], tk=cs)
                DMA('sp', cs[:, 1, :], sin_in[:, t0:t0 + 128], R=[cs], W=[cs], tk=cs)
            transpose_mod(xt, hT, w, on_dve=True)
            yield
            bk = ps_ring('B')
            proj_fm(hT, bk, [8] if is_ctx else [8, 9])
            for kc in range(8):
                MM(PS[:, bk, 256:384], hT[:, kc, :], WB[:, kc, 10 * 128:11 * 128], start=(kc == 0), stop=(kc == 7),
                   R=[WBk[kc], hT], W=[PSk[bk]])
            kslice = kTs(slice(0, 128), ti)
            if is_ctx:
                CP('act', kslice, PS[:, bk, 0:128], R=[PSk[bk]], W=[kTk[ti]])
            else:
                TT('dve', t1[:, 0, :], PS[:, bk, 0:128], cs[:, 0, :], ALU.mult, R=[PSk[bk], cs], W=[t1])
                TT('dve', t2[:, 0, :], PS[:, bk, 128:256], cs[:, 1, :], ALU.mult, R=[PSk[bk], cs], W=[t2])
                TT('pool', kslice, t1[:, 0, :], t2[:, 0, :], ALU.add, R=[t1, t2], W=[kTk[ti]])
            CP('act', Vx[:, ti, :].rearrange("p (a b) -> p a b", b=64)[:, 0::3, :],
               PS[:, bk, 256:384].rearrange("p (a b) -> p a b", b=64), R=[PSk[bk]], W=[Vxk[ti]])
            yield
            bxb = ps_ring('B')
            proj_fm(hT, bxb, [15, 16, 17, 18])
            CP('dve', xb_sb[:], bankv(bxb), R=[PSk[bxb]], W=[xb_sb])
            bcg = ps_ring('B')
            proj_fm(hT, bcg, [23, 24, 25, 26])
            TT('dve', U[:, :, 1:129], bankv(bcg), xb_sb[:], ALU.mult, R=[PSk[bcg], xb_sb], W=[U])
            first_of_seq = ti in (0, 2)
            Up = U_ring[(ti - 1) % 3]
            if first_of_seq:
                MEMSET('pool', U[:, :, 0:1], 0.0, W=[U])
                if ti == 2:
                    MEMSET('pool', Up[:, :, 129:130], 0.0, W=[Up])
            else:
                CP('pool', U[:, :, 0:1], Up[:, :, 128:129], R=[Up], W=[U])
                CP('pool', Up[:, :, 129:130], U[:, :, 1:2], R=[U], W=[Up])
            if ti == NTT - 1:
                MEMSET('pool', U[:, :, 129:130], 0.0, W=[U])
            yield

        def stage1b(ti):
            is_ctx = ti < 2
            hT = hT_ring[ti % 2]
            qf = qf_ring[ti % 2]
            sga = sga_ring[ti % 2]
            bg = bg_ring[ti % 2]
            sgb = sgb_ring[ti % 2]
            cs = cs_ring[ti % 2]
            bq = ps_ring('B')
            proj_fm(hT, bq, [0, 1, 2, 3])
            if is_ctx:
                CP('act', qf[:], bankv(bq), R=[PSk[bq]], W=[qf])
            else:
                yield
                br = ps_ring('B')
                proj_fm(hT, br, [4, 5, 6, 7])
                TT('dve', t1[:], bankv(bq), cs[:, 0:1, :].to_broadcast([128, 4, 128]), ALU.mult, R=[PSk[bq], cs], W=[t1])
                TT('dve', t2[:], bankv(br), cs[:, 1:2, :].to_broadcast([128, 4, 128]), ALU.mult, R=[PSk[br], cs], W=[t2])
                TT('pool', qf[:], t1[:], t2[:], ALU.add, R=[t1, t2], W=[qf])
            yield
            def silu_from_bank(dst, bank):
                ACT(dst[:], bankv(bank), AF.Exp, R=[PSk[bank]], W=[dst], scale=-1.0)
                ACT(dst[:], dst[:], AF.Ln, R=[dst], W=[dst], bias=one_col[:, 0:1])
                ACT(dst[:], dst[:], AF.Exp, R=[dst], W=[dst], scale=-1.0)
                TT('dve', dst[:], dst[:], bankv(bank), ALU.mult, R=[dst, PSk[bank]], W=[dst])

            bga = ps_ring('B')
            proj_fm(hT, bga, [11, 12, 13, 14])
            silu_from_bank(sga, bga)
            yield
            bbg = ps_ring('B')
            proj_fm(hT, bbg, [19, 20, 21, 22])
            CP('dve', bg[:], bankv(bbg), R=[PSk[bbg]], W=[bg])
            yield
            bgb = ps_ring('B')
            proj_fm(hT, bgb, [27, 28, 29, 30])
            silu_from_bank(sgb, bgb)
            yield

        def stage2(ti):
            is_ctx = ti < 2
            w = 1 if is_ctx else 0
            xt = xt_ring[ti % NXT]
            qf = qf_ring[ti % 2]
            sga = sga_ring[ti % 2]
            U = U_ring[ti % 3]
            bg = bg_ring[ti % 2]
            sgb = sgb_ring[ti % 2]
            chunks = [(0, None), (1, None)]
            if not is_ctx:
                if ti - 1 >= 2:
                    chunks.append((ti - 1, maskL))
                chunks.append((ti, None))
                if ti + 1 < NTT:
                    chunks.append((ti + 1, maskR))
            items = [(kt, mask, kvh) for (kt, mask) in chunks for kvh in range(2)]

            def score(it):
                kt, mask, kvh = it
                pr = slice(64 * kvh, 64 * kvh + 64)
                sbk = ps_ring('C')
                MM(PS[:, sbk, :], kTs(pr, kt), qf[pr, :, :], R=[kTk[kt], qf], W=[PSk[sbk]])
                return sbk

            def conv_fc(fc):
                TS('dve', cy[:, fc, :], U[:, fc, 0:128], cw[:, fc, 0:1], None, ALU.mult, ALU.bypass, R=[U, cw], W=[cy])
                STT(cy[:, fc, :], U[:, fc, 1:129], cw[:, fc, 1:2], cy[:, fc, :], ALU.mult, ALU.add, R=[U, cw, cy], W=[cy])
                STT(cy[:, fc, :], U[:, fc, 2:130], cw[:, fc, 2:3], cy[:, fc, :], ALU.mult, ALU.add, R=[U, cw, cy], W=[cy])

            assert len(items) >= 4
            pending = score(items[0])
            seen = set()
            for n, it in enumerate(items):
                kt, mask, kvh = it
                sbk = pending
                if n + 1 < len(items):
                    pending = score(items[n + 1])
                acc = 6 + kvh
                pt = PT_ring[pt_ctr[0] % 3]
                pt_ctr[0] += 1
                ACT(pt[:], PS[:, sbk, :], AF.Exp, R=[PSk[sbk]], W=[pt], scale=0.125)
                if mask is not None:
                    TT('pool', pt[:], pt[:], mask[:].rearrange("p a b -> p (a b)"), ALU.mult, R=[pt, mask], W=[pt])
                MM(PS[:, acc, :], Vx[:, kt, kvh * 128:(kvh + 1) * 128], pt[:], start=(kvh not in seen), stop=False,
                   R=[Vxk[kt], pt], W=[PSk[acc]])
                seen.add(kvh)
                if n < 4:
                    conv_fc(n)
                if n == 3:
                    TT('pool', cy[:], cy[:], bg[:], ALU.mult, R=[cy, bg], W=[cy])
                    TT('pool', bT[:], cy[:], sgb[:], ALU.mult, R=[cy, sgb], W=[bT])
                yield
            for kvh in range(2):
                acc = 6 + kvh
                MM(PS[:, acc, :], sel[0:1, kvh, :], esrow[0:1, kvh * 4:(kvh + 1) * 4, :].rearrange("p a b -> p (a b)"),
                   start=False, stop=True, R=[sel, esrow], W=[PSk[acc]])
            for kvh in range(2):
                acc = 6 + kvh
                pr = slice(64 * kvh, 64 * kvh + 64)
                dn = slice(64 * (1 - kvh), 64 * (1 - kvh) + 64)
                CP('dve', rec[pr, :], PS[dn, acc, :], R=[PSk[acc]], W=[rec])
                ACT(rec[pr, :], rec[pr, :], AF.Ln, R=[rec], W=[rec])
                ACT(rec[pr, :], rec[pr, :], AF.Exp, R=[rec], W=[rec], scale=-1.0)
                TT('dve', ot[pr, :], PS[pr, acc, :], rec[pr, :], ALU.mult, R=[PSk[acc], rec], W=[ot])
                TT('pool', aT[pr, :, :], ot[pr, :].rearrange("p (a b) -> p a b", a=4), sga[pr, :, :], ALU.mult,
                   R=[ot, sga], W=[aT])
                yield
            for nh in range(2):
                for fcc in range(8):
                    lhs = aT[:, fcc, :] if fcc < 4 else bT[:, fcc - 4, :]
                    MM(PS[:, nh, :], lhs, WOB[:, fcc, nh * 512:(nh + 1) * 512], start=(fcc == 0), stop=(fcc == 7),
                       R=[aT if fcc < 4 else bT, WOBk[fcc]], W=[PSk[nh]])
            epilogue(l, ti, (0, 1), xt, w)

        def chain(*gs):
            for g in gs:
                if g is not None:
                    yield from g

        run_gen(stage1a(0))
        run_gen(stage1b(0))
        run_gen(stage1a(1))
        for ti in range(1, NTT):
            interleave(chain(stage1b(ti), stage1a(ti + 1) if ti + 1 < NTT else None), stage2(ti - 1))
        run_gen(stage2(NTT - 1))

    def c_layer(l):
        j = l // 2
        ctx_out = l < DEPTH - 1
        layer_setup(l)
        NCH, CS, HC = 4, 32, 16
        xt_ring = [sb([128, D], F32, "cxt%d" % i) for i in range(2)]
        hT = sb([128, 8, 128], BF16, "chT")
        qs = sb([128, 8, 128], F32, "qs")
        sA = sb([128, 8, 128], F32, "sA")
        kk = sb([128, 8, 128], F32, "kk")
        sA2 = sb([128, 8, 128], F32, "sA2")
        kk2 = sb([128, 8, 128], F32, "kk2")
        sA_d = [sA, sA2]
        kk_d = [kk, kk2]
        E1 = sb([128, 8, 128], F32, "E1")
        E2 = sb([128, 8, 128], F32, "E2")
        Qp_r = [sb([128, 8, 128], BF16, "Qp%d" % i) for i in range(2)]
        Kp_r = [sb([128, 8, 128], BF16, "Kp%d" % i) for i in range(2)]
        Qpb = sb([128, 8, 128], BF16, "Qpb")
        Kpb = sb([128, 8, 128], BF16, "Kpb")
        Kz = sb([128, 8, 128], BF16, "Kz")
        vsb_r = [sb([128, D], BF16, "vsb%d" % i) for i in range(2)]
        sg = ep_ring[1]
        of = ep_ring[0]
        Ktok = sb([128, 8, 128], BF16, "Ktok")
        AT = sb([128, 8, CS], BF16, "AT")
        S_h = [sb([128, 4, 128], F32, "S%d" % i) for i in range(2)]
        SA_h = [sb([128, 4, 128], F32, "SA%d" % i) for i in range(2)]
        SAb_h = [sb([128, 4, 128], BF16, "SAb%d" % i) for i in range(2)]
        abF_r = [sb([128, 40], F32, "abF%d" % i) for i in range(2)]
        abB = sb([128, NTT, 40], F32, "abB")
        alf_d = [sb([128, 32], F32, "alf%d" % i) for i in range(2)]
        abB_k = [Tk("abB%d" % i) for i in range(NTT)]
        bet = sb([128, 32], F32, "bet")
        lbt = sb([128, 2, 8, 2], F32, "lbt")
        lb = sb([128, 2, 8], F32, "lb")
        oml = sb([128, 2, 8], F32, "oml")
        noml = sb([128, 2, 8], F32, "noml")
        gn_rep = sb([128, 128], F32, "gn_rep")
        maskF = sb([128, 8, CS], BF16, "maskF")
        maskB = sb([128, 8, CS], BF16, "maskB")
        hmF = sb([128, 128], BF16, "hmF")
        hmB = sb([128, 128], BF16, "hmB")
        rmF = sb([128, 8, 128], BF16, "rmF")
        rmB = sb([128, 8, 128], BF16, "rmB")
        ss = sb([128, 8], F32, "ss")
        mcol = sb([128, 32], F32, "mcol")
        gT = sb([128, 8, 128], BF16, "gT")
        tmpo = E1
        for d_ in range(2):
            DMA('sp', lbt[:, d_, :, :], lbcols[d_], R=[lbt], W=[lbt], tk=lbt)
        if j == 0:
            MEMSET('pool', lb[:], 0.0, W=[lb])
        else:
            TT('dve', lb[:], lbt[:, :, :, 1], lbt[:, :, :, 0], ALU.subtract, R=[lbt], W=[lb])
            ACT(lb[:], lb[:], AF.Sigmoid, R=[lb], W=[lb])
        TS('dve', oml[:], lb[:], -1.0, 1.0, ALU.mult, ALU.add, R=[lb], W=[oml])
        TS('dve', noml[:], oml[:], -1.0, None, ALU.mult, ALU.bypass, R=[oml], W=[noml])
        DMA('sp', gn_rep[:], gnorm_c[j].partition_broadcast(128), W=[gn_rep], tk=gn_rep)
        for m_, sign in ((maskF, 1), (maskB, -1)):
            CP('pool', m_[:], ones_b[:, 0:8 * CS].rearrange("p (a b) -> p a b", a=8), R=[ones_b], W=[m_])
            for qd in range(NCH):
                pr = slice(CS * qd, CS * qd + CS)
                P.op('pool', lambda e, m_=m_, pr=pr, sign=sign: e.affine_select(
                    out=m_[pr, :, :], in_=m_[pr, :, :], pattern=[[0, 8], [sign, CS]], compare_op=ALU.is_ge,
                    fill=0.0, base=0, channel_multiplier=-sign), ks([m_]), ks([m_]))

        def chv(ap):
            return ap.rearrange("p h (c t) -> p (h c) t", t=CS)

        MEMSET('pool', rmF[:], 1.0, W=[rmF])
        MEMSET('pool', rmB[:], 1.0, W=[rmB])
        MEMSET('pool', chv(rmF[:])[:, :, 0:1], 0.0, W=[rmF])
        MEMSET('pool', chv(rmB[:])[:, :, CS - 1:CS], 0.0, W=[rmB])
        MEMSET('pool', hmF[:], 0.0, W=[hmF])
        MEMSET('pool', hmB[:], 0.0, W=[hmB])
        MEMSET('pool', hmF[:].rearrange("p (c t) -> p c t", t=CS)[:, :, 0:HC], 1.0, W=[hmF])
        MEMSET('pool', hmB[:].rearrange("p (c t) -> p c t", t=CS)[:, :, HC:CS], 1.0, W=[hmB])

        def flat(b_):
            return b_[:].rearrange("p h t -> p (h t)")

        def zsig(d_, zbanks):
            for hb in range(2):
                ACT(sA_d[d_][:, hb * 4:hb * 4 + 4, :], PS[:, zbanks[hb], :].rearrange("p (a t) -> p a t", a=4), AF.Sigmoid,
                    R=[PSk[zbanks[hb]]], W=[sA_d[d_]])

        def prep_dir(ti, d_, Qo, Ko, abv, abk):
            sX, kX, alf = sA_d[d_], kk_d[d_], alf_d[d_]
            if j == 0:
                TS('pool', kX[:], sX[:], -1.0, 1.0, ALU.mult, ALU.add, R=[sX], W=[kX])
                yield
                ACT(sX[:], sX[:], AF.Ln, R=[sX], W=[sX])
            else:
                TT('dve', sX[:], sX[:], oml[:, d_, :].unsqueeze(2).to_broadcast([128, 8, 128]), ALU.mult, R=[sX, oml], W=[sX])
                TT('dve', sX[:], sX[:], lb[:, d_, :].unsqueeze(2).to_broadcast([128, 8, 128]), ALU.add, R=[sX, lb], W=[sX])
                yield
                TS('pool', kX[:], sX[:], -1.0, 1.0, ALU.mult, ALU.add, R=[sX], W=[kX])
                ACT(sX[:], sX[:], AF.Ln, R=[sX], W=[sX])
            yield
            if d_ == 0:
                P.op('dve', lambda e: e.tensor_tensor_scan(out=flat(sX), data0=flat(rmF), data1=flat(sX), initial=0.0,
                                                           op0=ALU.mult, op1=ALU.add), ks([sX, rmF]), ks([sX]))
                mid, last = HC - 1, CS - 1
            else:
                P.op('dve', lambda e: e.tensor_tensor_scan(out=flat(sX)[:, ::-1], data0=flat(rmB)[:, ::-1],
                                                           data1=flat(sX)[:, ::-1], initial=0.0,
                                                           op0=ALU.mult, op1=ALU.add), ks([sX, rmB]), ks([sX]))
                mid, last = HC, 0
            yield
            CP('dve', mcol[:], chv(sX[:])[:, :, mid], R=[sX], W=[mcol])
            ACT(alf[:], mcol[:], AF.Exp, R=[mcol], W=[alf])
            TT('dve', chv(sX[:]), chv(sX[:]), mcol[:].unsqueeze(2).to_broadcast([128, 32, CS]), ALU.subtract,
               R=[sX, mcol], W=[sX])
            yield
            ACT(E1[:], sX[:], AF.Exp, R=[sX], W=[E1])
            ACT(E2[:], sX[:], AF.Exp, R=[sX], W=[E2], scale=-1.0)
            CP('pool', bet[:], chv(E1[:])[:, :, last], R=[E1], W=[bet])
            al3 = alf[:].rearrange("p (h c) -> p h c", c=NCH)
            ga3 = abv[:, 8:40].rearrange("p (h c) -> p h c", c=NCH)
            be3 = bet[:].rearrange("p (h c) -> p h c", c=NCH)
            if d_ == 0:
                TT('dve', ga3[:, :, 0:NCH - 1], be3[:, :, 0:NCH - 1], al3[:, :, 1:NCH], ALU.mult, R=[bet, alf], W=[abk])
                CP('dve', ga3[:, :, NCH - 1:NCH], be3[:, :, NCH - 1:NCH], R=[bet], W=[abk])
                CP('dve', abv[:, 0:8], al3[:, :, 0], R=[alf], W=[abk])
            else:
                TT('dve', ga3[:, :, 1:NCH], be3[:, :, 1:NCH], al3[:, :, 0:NCH - 1], ALU.mult, R=[bet, alf], W=[abk])
                CP('dve', ga3[:, :, 0:1], be3[:, :, 0:1], R=[bet], W=[abk])
                CP('dve', abv[:, 0:8], al3[:, :, NCH - 1], R=[alf], W=[abk])
            STT(Qo[:], qs[:], 128.0 ** -0.5, E1[:], ALU.mult, ALU.mult, R=[qs, E1], W=[Qo])
            TT('pool', Ko[:], kX[:], E2[:], ALU.mult, R=[kX, E2], W=[Ko])
            yield

        def izip(*gens):
            gens = [g for g in gens if g is not None]
            while gens:
                for g in list(gens):
                    try:
                        next(g)
                        yield
                    except StopIteration:
                        gens.remove(g)

        def scan_tile(ti, d_, Qo, Ko, vsb, want_out, abv, abk):
            afirst = abv[:, 0:8]
            ga3 = abv[:, 8:40].rearrange("p (h c) -> p h c", c=NCH)
            bt = ps_ring('A')
            ptb = PS[:, bt, :].bitcast(BF16).rearrange("p (h t) -> p h t", h=8)
            for h in range(8):
                TR(ptb[:, h, :], Ko[:, h, :], identb[:], R=[Ko, identb], W=[PSk[bt]])
            CP('act', Ktok[:], ptb, R=[PSk[bt]], W=[Ktok])
            yield
            if want_out:
                TT('pool', Kz[:], Ko[:], (hmF if d_ == 0 else hmB)[:].unsqueeze(1).to_broadcast([128, 8, 128]), ALU.mult,
                   R=[Ko, hmF, hmB], W=[Kz])
                ba = ps_ring('A')
                pa = PS[:, ba, 0:8 * CS].rearrange("p (h c) -> p h c", h=8)
                for cj in range(NCH):
                    pr = slice(CS * cj, CS * cj + CS)
                    c0 = CS * cj
                    for h in range(8):
                        if d_ == 0:
                            MM(pa[pr, h, 0:HC], Kz[:, h, pr], Qo[:, h, c0:c0 + HC], R=[Kz, Qo], W=[PSk[ba]], tp=(0, c0))
                            MM(pa[pr, h, HC:CS], Ko[:, h, pr], Qo[:, h, c0 + HC:c0 + CS], R=[Ko, Qo], W=[PSk[ba]], tp=(0, c0))
                        else:
                            MM(pa[pr, h, HC:CS], Kz[:, h, pr], Qo[:, h, c0 + HC:c0 + CS], R=[Kz, Qo], W=[PSk[ba]], tp=(0, c0))
                            MM(pa[pr, h, 0:HC], Ko[:, h, pr], Qo[:, h, c0:c0 + HC], R=[Ko, Qo], W=[PSk[ba]], tp=(0, c0))
                TT('dve', AT[:], pa, (maskF if d_ == 0 else maskB)[:], ALU.mult, R=[PSk[ba], maskF, maskB], W=[AT])
            order = list(range(NCH)) if d_ == 0 else list(range(NCH - 1, -1, -1))
            for n_, cj in enumerate(order):
                pr = slice(CS * cj, CS * cj + CS)
                if n_ == 0:
                    for hf in range(2):
                        TT('pool', SA_h[hf][:], S_h[hf][:], afirst[:, 4 * hf:4 * hf + 4].unsqueeze(2).to_broadcast([128, 4, 128]),
                           ALU.mult, R=[S_h[hf], abk], W=[SA_h[hf]])
                if want_out:
                    for hf in range(2):
                        CP('act', SAb_h[hf][:], SA_h[hf][:], R=[SA_h[hf]], W=[SAb_h[hf]])
                for hf in range(2):
                    ob = 6 + hf
                    if want_out:
                        for hh in range(4):
                            h = 4 * hf + hh
                            oo = PS[pr, ob, hh * 128:(hh + 1) * 128]
                            MM(oo, Qo[:, h, pr], SAb_h[hf][:, hh, :], start=(hh == 0), stop=False, R=[Qo, SAb_h[hf]], W=[PSk[ob]],
                               tp=(0, CS * cj), sgc=True)
                        for hh in range(4):
                            h = 4 * hf + hh
                            oo = PS[pr, ob, hh * 128:(hh + 1) * 128]
                            MM(oo, AT[pr, h, :], vsb[pr, h * 128:(h + 1) * 128], start=False, stop=True, R=[AT, vsb], W=[PSk[ob]],
                               tp=(CS * cj, CS * cj), sgc=True)
                    pb_ = 4 + hf
                    for hh in range(4):
                        h = 4 * hf + hh
                        MM(PS[:, pb_, hh * 128:(hh + 1) * 128], Ktok[pr, h, :], vsb[pr, h * 128:(h + 1) * 128],
                           R=[Ktok, vsb], W=[PSk[pb_]], tp=(CS * cj, 0))
                yield
                for hf in range(2):
                    TT('dve', SA_h[hf][:], SA_h[hf][:], PS[:, 4 + hf, :].rearrange("p (h t) -> p h t", h=4), ALU.add,
                       R=[SA_h[hf], PSk[4 + hf]], W=[SA_h[hf]])
                for hf in range(2):
                    dst = S_h[hf] if n_ == NCH - 1 else SA_h[hf]
                    TT('pool', dst[:], SA_h[hf][:], ga3[:, 4 * hf:4 * hf + 4, cj:cj + 1].to_broadcast([128, 4, 128]), ALU.mult,
                       R=[SA_h[hf], abk], W=[dst])
                yield

        for hf in range(2):
            MEMSET('pool', S_h[hf][:], 0.0, W=[S_h[hf]])

        def front(ti):
            is_ctx = ti < 2
            w = 1 if is_ctx else 0
            xt = xt_ring[ti % 2]
            vsb = vsb_r[ti % 2]
            load_x(l, ti, xt)
            transpose_mod(xt, hT, w)
            yield

            def proj_fm(bank, c0):
                for s_ in range(4):
                    cc = c0 + s_
                    for kc in range(8):
                        MM(PS[:, bank, s_ * 128:(s_ + 1) * 128], WB[:, kc, cc * 128:(cc + 1) * 128], hT[:, kc, :],
                           start=(kc == 0), stop=(kc == 7), R=[WBk[kc], hT], W=[PSk[bank]])

            def proj_tm(bank, c0):
                for kc in range(8):
                    MM(PS[:, bank, :], hT[:, kc, :], WB[:, kc, c0 * 128:(c0 + 4) * 128], start=(kc == 0), stop=(kc == 7),
                       R=[WBk[kc], hT], W=[PSk[bank]])

            def zproj(d_):
                for hb in range(2):
                    b = ps_ring('B')
                    proj_fm(b, 8 + d_ * 8 + hb * 4)
                    ACT(sA_d[d_][:, hb * 4:hb * 4 + 4, :], PS[:, b, :].rearrange("p (a t) -> p a t", a=4), AF.Sigmoid,
                        R=[PSk[b]], W=[sA_d[d_]])
                    yield

            def qproj():
                for hb in range(2):
                    b = ps_ring('B')
                    proj_fm(b, hb * 4)
                    ACT(qs[:, hb * 4:hb * 4 + 4, :], PS[:, b, :].rearrange("p (a t) -> p a t", a=4), AF.Silu, R=[PSk[b]], W=[qs])
                    yield

            def vgproj():
                for hb in range(2):
                    b = ps_ring('B')
                    proj_tm(b, 24 + hb * 4)
                    CP('act', vsb[:, hb * 512:(hb + 1) * 512], PS[:, b, :], R=[PSk[b]], W=[vsb])
                    yield
                for hb in range(2):
                    b = ps_ring('B')
                    proj_tm(b, 32 + hb * 4)
                    ACT(sg[:, hb * 512:(hb + 1) * 512], PS[:, b, :], AF.Silu, R=[PSk[b]], W=[sg])
                    yield

            def chain2(*gs):
                for g in gs:
                    yield from g

            yield from zproj(0)
            yield from qproj()
            yield from izip(prep_dir(ti, 0, Qp_r[ti % 2], Kp_r[ti % 2], abF_r[ti % 2][:, :], abF_r[ti % 2].k),
                            chain2(zproj(1), prep_dir(ti, 1, Qpb, Kpb, abB[:, ti, :], abB_k[ti])),
                            vgproj())
            want = ctx_out or not is_ctx
            if want:
                DMA('sp', sc_sg[ti], sg[:], R=[sg], W=[sck[ti]], tk=sg)
                DMA('sp', sc_q[ti], flat(Qpb), R=[Qpb], W=[sck[ti]], tk=Qpb)
            DMA('sp', sc_k[ti], flat(Kpb), R=[Kpb], W=[sck[ti]], tk=Kpb)
            DMA('sp', sc_v[ti], vsb[:], R=[vsb], W=[sck[ti]], tk=vsb)

        def scan1(ti):
            is_ctx = ti < 2
            want = ctx_out or not is_ctx
            yield from scan_tile(ti, 0, Qp_r[ti % 2], Kp_r[ti % 2], vsb_r[ti % 2], want, abF_r[ti % 2][:, :], abF_r[ti % 2].k)
            if want:
                CP('act', of[:], PS[:, 6:8, :].rearrange("p a b -> p (a b)"), R=[PSk[6], PSk[7]], W=[of])
                DMA('sp', sc_of[ti], of[:], R=[of], W=[sck[ti]], tk=of)

        run_gen(front(0))
        for ti in range(NTT):
            interleave(front(ti + 1) if ti + 1 < NTT else None, scan1(ti))

        for hf in range(2):
            MEMSET('pool', S_h[hf][:], 0.0, W=[S_h[hf]])
        order = [1, 0] + list(range(NTT - 1, 1, -1))

        def f2(b_):
            return b_[:] if len(b_[:].shape) == 2 else b_[:].rearrange("p h t -> p (h t)")

        Kpb_s = [Kpb, Kp_r[0]]
        Qpb_s = [Qpb, Qp_r[0]]
        of_s = [sA2, qs]
        sg_s = [kk2, sA]
        tmpo_s = [E1, kk]

        def wants(ti):
            return ctx_out or ti >= 2

        def p2_load(n_):
            ti = order[n_]
            sl = n_ % 2
            DMA('sp', f2(Kpb_s[sl]), sc_k[ti], R=[sck[ti]], W=[Kpb_s[sl]], tk=Kpb_s[sl])
            DMA('sp', vsb_r[sl][:], sc_v[ti], R=[sck[ti]], W=[vsb_r[sl]], tk=vsb_r[sl])
            yield
            if wants(ti):
                DMA('sp', f2(Qpb_s[sl]), sc_q[ti], R=[sck[ti]], W=[Qpb_s[sl]], tk=Qpb_s[sl])
                yield

        def p2_load_b(n_):
            ti = order[n_]
            sl = n_ % 2
            if wants(ti):
                load_x(l, ti, xt_ring[sl])
                DMA('sp', f2(of_s[sl]), sc_of[ti], R=[sck[ti]], W=[of_s[sl]], tk=of_s[sl])
                DMA('sp', f2(sg_s[sl]), sc_sg[ti], R=[sck[ti]], W=[sg_s[sl]], tk=sg_s[sl])
            yield

        def p2_scan(n_):
            ti = order[n_]
            sl = n_ % 2
            yield from scan_tile(ti, 1, Qpb_s[sl], Kpb_s[sl], vsb_r[sl], wants(ti), abB[:, ti, :], abB_k[ti])
            if wants(ti):
                TT('dve', f2(tmpo_s[sl]), PS[:, 6:8, :].rearrange("p a b -> p (a b)"), f2(of_s[sl]), ALU.add,
                   R=[PSk[6], PSk[7], of_s[sl]], W=[tmpo_s[sl]])
            yield

        def p2_read(n_):
            ti = order[n_]
            sl = n_ % 2
            if not wants(ti):
                return
            w = 1 if ti < 2 else 0
            tmpo, of_, sg_ = tmpo_s[sl], of_s[sl], sg_s[sl]
            ACT(f2(of_), f2(tmpo), AF.Square, R=[tmpo], W=[of_])
            P.op('dve', lambda e: e.tensor_reduce(out=ss[:], in_=f2(of_).rearrange("p (h t) -> p h t", h=8), axis=AX.X, op=ALU.add),
                 ks([of_]), ks([ss]))
            TS('dve', ss[:], ss[:], 1.0 / 128.0, RMS_EPS, ALU.mult, ALU.add, R=[ss], W=[ss])
            TT('pool', ss[:], ss[:], mhalf[:], ALU.pow, R=[ss, mhalf], W=[ss])
            yield
            t3 = f2(tmpo).rearrange("p (h t) -> p h t", h=8)
            TT('dve', t3, t3, ss[:].unsqueeze(2).to_broadcast([128, 8, 128]), ALU.mult, R=[tmpo, ss], W=[tmpo])
            TT('dve', t3, t3, gn_rep[:].unsqueeze(1).to_broadcast([128, 8, 128]), ALU.mult, R=[tmpo, gn_rep], W=[tmpo])
            yield
            TT('dve', f2(of_), f2(tmpo), f2(sg_), ALU.mult, R=[tmpo, sg_], W=[of_])
            yield
            for half in range(2):
                b = ps_ring('A')
                for s_ in range(4):
                    h = half * 4 + s_
                    TR(PS[:, b, s_ * 128:(s_ + 1) * 128], f2(of_)[:, h * 128:(h + 1) * 128], ident[:], R=[of_, ident], W=[PSk[b]])
                CP('act', gT[:, half * 4:half * 4 + 4, :], PS[:, b, :].rearrange("p (a t) -> p a t", a=4), R=[PSk[b]], W=[gT])
                yield
            for nh in range(2):
                for h in range(8):
                    MM(PS[:, 2 + nh, :], gT[:, h, :], WOB[:, h, nh * 512:(nh + 1) * 512], start=(h == 0), stop=(h == 7),
                       R=[gT, WOBk[h]], W=[PSk[2 + nh]])
                yield
            epilogue(l, ti, (2, 3), xt_ring[sl], w)

        NO = len(order)
        run_gen(p2_load(0))
        run_gen(p2_load_b(0))
        for n_ in range(NO + 1):
            interleave(p2_scan(n_) if n_ < NO else None,
                       p2_read(n_ - 1) if n_ >= 1 else None,
                       p2_load(n_ + 1) if n_ + 1 < NO else None)
            if n_ + 1 < NO:
                run_gen(p2_load_b(n_ + 1))

    for l in layers:
        with ExitStack() as es:
            cur_stack[0] = es
            if l % 2 == 0:
                ab_layer(l)
            else:
                c_layer(l)
            barrier()
            cur_stack[0] = None
    fin = []
    seen = set()
    for r in out_stores:
        if id(r) not in seen:
            seen.add(id(r))
            fin.append(r)
    P.op('sp', lambda e: e.nop(), (), ks(fin))
    P.emit()
    return nc


def _rope_tables(L):
    half = 16
    freqs = 10000.0 ** (-np.arange(half, dtype=np.float32) / half)
    t = np.arange(L)
    row = (t // 64).astype(np.float32)
    col = (t % 64).astype(np.float32)
    cos_t = np.zeros((128, L), np.float32)
    sin_t = np.zeros((128, L), np.float32)
    for p in range(128):
        d = p % 64
        if d < 32:
            ang = row * freqs[d % 16]
            sgn = -1.0 if d < 16 else 1.0
        else:
            ang = col * freqs[(d - 32) % 16]
            sgn = -1.0 if (d - 32) < 16 else 1.0
        cos_t[p] = np.cos(ang.astype(np.float32))
        sin_t[p] = sgn * np.sin(ang.astype(np.float32))
    return cos_t, sin_t


def _ab_col_perm():
    rot = np.zeros(64, np.int64)
    for d in range(64):
        base = 0 if d < 32 else 32
        dd = d - base
        rot[d] = base + (dd + 16 if dd < 16 else dd - 16)
    qcols, qrot = [], []
    for jj in range(4):
        for h in (jj, 4 + jj):
            qcols += [h * 64 + d for d in range(64)]
            qrot += [h * 64 + int(rot[d]) for d in range(64)]
    kcols = [512 + i for i in range(128)]
    krot = [512 + kv * 64 + int(rot[d]) for kv in range(2) for d in range(64)]
    vcols = [640 + i for i in range(128)]
    gacols = [768 + c for c in qcols]
    rest = list(range(1280, 3328))
    cols = qcols + qrot + kcols + krot + vcols + gacols + rest
    assert len(cols) == AB_W
    rows_out = qcols + list(range(512, 1024))
    return np.array(cols), np.array(rows_out)


def make_in_maps(inputs, L=4096, nb=8):
    f = lambda a: np.ascontiguousarray(np.asarray(a, dtype=np.float32))
    x = f(inputs["x"]); c = f(inputs["c"]); ctx = f(inputs["ctx"]); c_ctx = f(inputs["c_ctx"])
    cols, rows_out = _ab_col_perm()
    w_in_ab = f(f(inputs["w_in_ab"])[:, :, cols])
    w_out_ab = f(f(inputs["w_out_ab"])[:, rows_out, :])
    conv = f(inputs["conv_ab"])
    convc = f(conv.reshape(2, 3, 4, 128).transpose(0, 3, 2, 1))
    lbc = f(inputs["lb_c"])
    lbcols = f(lbc.reshape(2, 2, 8, 128).transpose(0, 3, 2, 1))
    cos_t, sin_t = _rope_tables(L)
    shared = dict(
        w_ada=f(inputs["w_ada"]), bcols=f(f(inputs["b_ada"]).reshape(DEPTH, 24, 128).transpose(0, 2, 1)),
        ln_g=f(inputs["ln_g"]).reshape(DEPTH, 1, D), ln_b=f(inputs["ln_b"]).reshape(DEPTH, 1, D),
        w_in_ab=w_in_ab, w_out_ab=w_out_ab, sink_ab=f(inputs["sink_ab"]).reshape(2, 1, 8), convc=convc,
        w_in_c=f(inputs["w_in_c"]), w_out_c=f(inputs["w_out_c"]), lbcols=lbcols,
        gnorm_c=f(inputs["gnorm_c"]).reshape(2, 1, 128), cos_t=cos_t, sin_t=sin_t)
    maps = []
    for b in range(nb):
        cc = np.stack([c[b], c_ctx], -1).reshape(8, 128, 2).transpose(1, 0, 2)
        m = dict(shared)
        m.update(x=f(x[b]), ctx=f(ctx[b]), ccols=f(cc))
        maps.append(m)
    return maps


_CACHE = {}


def kernel(**inputs):
    L = int(np.asarray(inputs["x"]).shape[1])
    nb = int(np.asarray(inputs["x"]).shape[0])
    if L not in _CACHE:
        _CACHE[L] = build_program(L)
    nc = _CACHE[L]
    maps = make_in_maps(inputs, L, nb)
    res = run_bass_kernel_spmd(nc, maps, core_ids=list(range(nb)))
    return np.stack([np.asarray(r["out"]) for r in res.results], 0).astype(np.float32)
```

```python
import math
from contextlib import ExitStack
import numpy as np
import concourse.bass as bass
import concourse.mybir as mybir
from concourse.bass_utils import run_bass_kernel_spmd

AF = mybir.ActivationFunctionType
ALU = mybir.AluOpType
F32 = mybir.dt.float32
BF16 = mybir.dt.bfloat16
AX = mybir.AxisListType

D = 1024
CTX = 256
DEPTH = 4
ALPHA = (2 * DEPTH) ** 0.25
LN_EPS = 1e-5
RMS_EPS = 1e-6
AB_W = 3968
C_W = 5120

QUEUES = ['pe', 'act', 'dve', 'pool', 'sp']
SAME_ENG_SYNC = {'pe': False, 'act': True, 'dve': True, 'pool': True, 'sp': False}


class Tk:
    __slots__ = ('name', 'w', 'r', 'sem', 'semv')

    def __init__(self, name):
        self.name = name
        self.w = None
        self.r = []
        self.sem = None
        self.semv = 0


class Prog:
    def __init__(self, nc):
        self.nc = nc
        self.ops = {q: [] for q in QUEUES}

    def op(self, q, fn, reads=(), writes=(), dma=None):
        deps = []
        for t in reads:
            if t.w is not None:
                deps.append(t.w)
        for t in writes:
            if t.w is not None:
                deps.append(t.w)
            deps.extend(t.r)
        idx = len(self.ops[q])
        if dma is not None:
            if dma.sem is None:
                dma.sem = self.nc.alloc_semaphore("d_" + dma.name)
            dma.semv += 16
            tok = ('dma', dma, dma.semv)
        else:
            tok = ('eng', q, idx)
        self.ops[q].append(dict(fn=fn, deps=deps, tok=tok))
        for t in reads:
            if tok[0] == 'eng':
                t.r = [x for x in t.r if not (x[0] == 'eng' and x[1] == q)]
            t.r.append(tok)
        for t in writes:
            t.w = tok
            t.r = []
        return tok

    def emit(self):
        nc = self.nc
        sems = {q: nc.alloc_semaphore("q_" + q) for q in QUEUES}
        needed = {q: set() for q in QUEUES}
        for q in QUEUES:
            waited = {}
            for idx, o in enumerate(self.ops[q]):
                req = {}
                for d in o['deps']:
                    if d[0] == 'eng':
                        if d[1] == q and not SAME_ENG_SYNC[q]:
                            continue
                        key = ('eng', d[1])
                    else:
                        key = ('dma', id(d[1]), d[1])
                    if d[2] > req.get(key, -1):
                        req[key] = d[2]
                waits = []
                for key, val in req.items():
                    if waited.get(key, -1) >= val:
                        continue
                    waited[key] = val
                    waits.append((key, val))
                    if key[0] == 'eng':
                        needed[key[1]].add(val)
                o['waits'] = waits
        ms = {}
        for q in QUEUES:
            ms[q] = {idx: rank + 1 for rank, idx in enumerate(sorted(needed[q]))}
        with nc.Block() as block:
            def body(q):
                def f(e):
                    for idx, o in enumerate(self.ops[q]):
                        for key, val in o['waits']:
                            if key[0] == 'eng':
                                e.wait_ge(sems[key[1]], ms[key[1]][val])
                            else:
                                e.wait_ge(key[2].sem, val)
                        ins = o['fn'](e)
                        tok = o['tok']
                        if tok[0] == 'dma':
                            ins.then_inc(tok[1].sem, 16)
                        elif idx in ms[q]:
                            ins.then_inc(sems[q], 1)
                return f
            block.tensor(body('pe'))
            block.scalar(body('act'))
            block.vector(body('dve'))
            block.gpsimd(body('pool'))
            block.sync(body('sp'))


def interleave(*gens):
    gens = [g for g in gens if g is not None]
    while gens:
        for g in list(gens):
            try:
                next(g)
            except StopIteration:
                gens.remove(g)


def run_gen(g):
    for _ in g:
        pass


class Buf:
    def __init__(self, t, name):
        self.t = t
        self.k = Tk(name)

    def __getitem__(self, key):
        return self.t[key]


def build_program(L=4096, layers=(0, 1, 2, 3), dbg=False):
    NT = L // 128
    NTT = NT + 2
    nc = bass.Bass("TRN2", target_bir_lowering=False)
    P = Prog(nc)
    _n = [0]

    cur_stack = [None]

    def sb(shape, dtype, name=None):
        _n[0] += 1
        name = (name or "b") + "_%d" % _n[0]
        if cur_stack[0] is not None:
            return Buf(cur_stack[0].enter_context(nc.sbuf_tensor(name, list(shape), dtype)), name)
        return Buf(nc.alloc_sbuf_tensor(name, list(shape), dtype), name)

    def din(name, shape, dtype=F32):
        return nc.dram_tensor(name, list(shape), dtype, kind="ExternalInput").ap()

    def dscr(name, shape, dtype=F32):
        return nc.dram_tensor(name, list(shape), dtype).ap()

    x_in = din("x", [L, D])
    ctx_in = din("ctx", [CTX, D])
    ccols_in = din("ccols", [128, 8, 2])
    w_ada = din("w_ada", [DEPTH, D, 3 * D])
    bcols_in = din("bcols", [DEPTH, 128, 24])
    ln_g = din("ln_g", [DEPTH, 1, D])
    ln_b = din("ln_b", [DEPTH, 1, D])
    w_in_ab = din("w_in_ab", [2, D, AB_W])
    w_out_ab = din("w_out_ab", [2, D, D])
    sink_ab = din("sink_ab", [2, 1, 8])
    convc = din("convc", [2, 128, 4, 3])
    w_in_c = din("w_in_c", [2, D, C_W])
    w_out_c = din("w_out_c", [2, D, D])
    lbcols = din("lbcols", [2, 128, 8, 2])
    gnorm_c = din("gnorm_c", [2, 1, 128])
    cos_in = din("cos_t", [128, L])
    sin_in = din("sin_t", [128, L])
    out_d = nc.dram_tensor("out", [L, D], F32, kind="ExternalOutput").ap()
    xs = [dscr("xs0", [NTT * 128, D]), dscr("xs1", [NTT * 128, D])]
    sc_q = dscr("sc_q", [NTT, 128, 1024], BF16)
    sc_k = dscr("sc_k", [NTT, 128, 1024], BF16)
    sc_v = dscr("sc_v", [NTT, 128, 1024], BF16)
    sc_of = dscr("sc_of", [NTT, 128, 1024], F32)
    sc_sg = dscr("sc_sg", [NTT, 128, 1024], F32)
    xsk = [[Tk("xs%d_%d" % (s_, i)) for i in range(NTT)] for s_ in range(2)]
    sck = [Tk("sc_%d" % i) for i in range(NTT)]
    out_stores = []

    def ks(bufs):
        return [b.k if isinstance(b, Buf) else b for b in bufs]

    def MM(out, lhsT, rhs, start=True, stop=True, R=(), W=(), tp=None, sgc=False):
        kw = {}
        if tp is not None:
            kw['tile_position'] = tp
        if sgc:
            kw['skip_group_check'] = True
        P.op('pe', lambda e: e.matmul(out=out, lhsT=lhsT, rhs=rhs, start=start, stop=stop, **kw), ks(R), ks(W))

    def TR(out, in_, ident, R=(), W=()):
        P.op('pe', lambda e: e.transpose(out=out, in_=in_, identity=ident), ks(R), ks(W))

    def ACT(out, in_, func, R=(), W=(), scale=None, bias=None):
        kw = {}
        if scale is not None:
            kw['scale'] = scale
        if bias is not None:
            kw['bias'] = bias
        P.op('act', lambda e: e.activation(out=out, in_=in_, func=func, **kw), ks(R), ks(W))

    def TT(q, out, in0, in1, op, R=(), W=()):
        P.op(q, lambda e: e.tensor_tensor(out=out, in0=in0, in1=in1, op=op), ks(R), ks(W))

    def TS(q, out, in0, s1, s2, op0, op1, R=(), W=()):
        P.op(q, lambda e: e.tensor_scalar(out=out, in0=in0, scalar1=s1, scalar2=s2, op0=op0, op1=op1), ks(R), ks(W))

    def STT(out, in0, scalar, in1, op0, op1, R=(), W=()):
        P.op('dve', lambda e: e.scalar_tensor_tensor(out=out, in0=in0, scalar=scalar, in1=in1, op0=op0, op1=op1),
             ks(R), ks(W))

    def CP(q, out, in_, R=(), W=()):
        if q == 'act':
            P.op('act', lambda e: e.copy(out=out, in_=in_), ks(R), ks(W))
        else:
            P.op(q, lambda e: e.tensor_copy(out=out, in_=in_), ks(R), ks(W))

    def MEMSET(q, ap, val, W=()):
        P.op(q, lambda e: e.memset(ap, val), (), ks(W))

    all_dma_tk = []

    def DMA(q, out, in_, R=(), W=(), tk=None, mld=None):
        tkk = tk.k if isinstance(tk, Buf) else tk
        if tkk not in all_dma_tk:
            all_dma_tk.append(tkk)
        kw = {}
        if mld is not None:
            kw['max_dma_last_dim'] = mld
        return P.op(q, lambda e: e.dma_start(out=out, in_=in_, **kw), ks(R), ks(W), dma=tkk)

    def barrier():
        deps = []
        for q in ['pe', 'act', 'dve', 'pool']:
            if P.ops[q]:
                deps.append(('eng', q, len(P.ops[q]) - 1))
        for t in all_dma_tk:
            if t.sem is not None:
                deps.append(('dma', t, t.semv))
        for q in QUEUES:
            P.ops[q].append(dict(fn=lambda e: e.nop(), deps=list(deps), tok=('eng', q, len(P.ops[q]))))

    ident = sb([128, 128], F32, "ident")
    identb = sb([128, 128], BF16, "identb")
    ones_f = sb([128, 128], F32, "ones_f")
    ones_b = sb([128, 512], BF16, "ones_b")
    MEMSET('pool', ident[:], 0.0, W=[ident])
    P.op('pool', lambda e: e.affine_select(out=ident[:], in_=ident[:], pattern=[[-1, 128]], compare_op=ALU.not_equal,
                                           fill=1.0, base=0, channel_multiplier=1), ks([ident]), ks([ident]))
    CP('pool', identb[:], ident[:], R=[ident], W=[identb])
    MEMSET('pool', ones_f[:], 1.0, W=[ones_f])
    MEMSET('pool', ones_b[:], 1.0, W=[ones_b])
    eps_ln = sb([128, 1], F32, "eps_ln")
    MEMSET('pool', eps_ln[:], LN_EPS, W=[eps_ln])
    one_col = sb([128, 1], F32, "one_col")
    MEMSET('pool', one_col[:], 1.0, W=[one_col])
    mhalf = sb([128, 8], F32, "mhalf")
    MEMSET('pool', mhalf[:], -0.5, W=[mhalf])

    WB = sb([128, 8, C_W], BF16, "WB")
    WBk = [Tk("WBk%d" % i) for i in range(8)]
    WOB = sb([128, 8, D], BF16, "WOB")
    WOBk = [Tk("WOBk%d" % i) for i in range(8)]
    PS = nc.alloc_psum_tensor("PS", [128, 8, 512], F32)
    PSk = [Tk("ps%d" % i) for i in range(8)]
    ring_ctr = {'A': 0, 'B': 0, 'C': 0}

    def ps_ring(role):
        base = {'A': 0, 'B': 2, 'C': 4}[role]
        i = base + ring_ctr[role] % 2
        ring_ctr[role] += 1
        return i

    ccols = sb([128, 8, 2], F32, "ccols")
    scols = sb([128, 8, 2], F32, "scols")
    DMA('sp', ccols[:], ccols_in, W=[ccols], tk=ccols)
    ACT(scols[:], ccols[:], AF.Silu, R=[ccols], W=[scols])
    bcol = sb([128, 24], F32, "bcol")
    modall = sb([128, 24, 2], F32, "modall")
    modc = modall
    gate_rep = [sb([128, D], F32, "gate%d" % i) for i in range(2)]
    lng_rep = sb([128, D], F32, "lng")
    lnb_rep = sb([128, D], F32, "lnb")
    dg_ring = [sb([128, 128], F32, "dg%d" % i) for i in range(2)]
    ep_ring = [sb([128, D], F32, "ep%d" % i) for i in range(2)]
    wada_ctr = [0]

    def layer_setup(l):
        if l % 2 == 0:
            wi, wo, Wd = w_in_ab[l // 2], w_out_ab[l // 2], AB_W
        else:
            wi, wo, Wd = w_in_c[l // 2], w_out_c[l // 2], C_W
        for kc in range(8):
            DMA('pool', WB[:, kc, 0:Wd], wi[kc * 128:(kc + 1) * 128, :], W=[WBk[kc]], tk=WBk[kc], mld=4096)
        for kc in range(8):
            DMA('pool', WOB[:, kc, :], wo[kc * 128:(kc + 1) * 128, :], W=[WOBk[kc]], tk=WOBk[kc], mld=4096)
        DMA('sp', bcol[:], bcols_in[l], W=[bcol], tk=bcol)
        DMA('sp', lng_rep[:], ln_g[l].partition_broadcast(128), W=[lng_rep], tk=lng_rep)
        DMA('sp', lnb_rep[:], ln_b[l].partition_broadcast(128), W=[lnb_rep], tk=lnb_rep)
        for cc in range(24):
            wab = ep_ring[wada_ctr[0] % 2]
            wada_ctr[0] += 1
            wa = wab[:, :].rearrange("p (kc n) -> p kc n", kc=8)
            DMA('sp', wa, w_ada[l][:, cc * 128:(cc + 1) * 128].rearrange("(kc p) n -> p kc n", p=128), W=[wab], tk=wab)
            o = PS[:, 0, cc * 2:cc * 2 + 2]
            for kc in range(8):
                MM(o, wa[:, kc, :], scols[:, kc, :], start=(kc == 0), stop=(kc == 7), R=[wab, scols], W=[PSk[0]])
        TT('dve', modall[:], PS[:, 0, 0:48].rearrange("p (c w) -> p c w", w=2),
           bcol[:].unsqueeze(2).to_broadcast([128, 24, 2]), ALU.add, R=[PSk[0], bcol], W=[modall])
        TS('dve', modall[:, 8:16, :], modall[:, 8:16, :], 1.0, None, ALU.add, ALU.bypass, R=[modall], W=[modall])
        dgc = 0
        for w in range(2):
            for half in range(2):
                for j_ in range(4):
                    kc = half * 4 + j_
                    dg = dg_ring[dgc % 2]
                    dgc += 1
                    TS('dve', dg[:], ident[:], modall[:, 16 + kc, w:w + 1], None, ALU.mult, ALU.bypass,
                       R=[ident, modall], W=[dg])
                    MM(PS[:, 1, j_ * 128:(j_ + 1) * 128], ones_f[:], dg[:], R=[ones_f, dg], W=[PSk[1]])
                CP('dve', gate_rep[w][:, half * 512:(half + 1) * 512], PS[:, 1, :], R=[PSk[1]], W=[gate_rep[w]])

    def src_ap(l, ti):
        if l == layers[0]:
            if ti < 2:
                return ctx_in[ti * 128:(ti + 1) * 128, :], None
            return x_in[(ti - 2) * 128:(ti - 1) * 128, :], None
        s = (layers.index(l) + 1) % 2
        return xs[s][ti * 128:(ti + 1) * 128, :], xsk[s][ti]

    def dst_ap(l, ti):
        if l == layers[-1]:
            if ti < 2:
                return None, None
            return out_d[(ti - 2) * 128:(ti - 1) * 128, :], None
        s = (layers.index(l)) % 2
        return xs[s][ti * 128:(ti + 1) * 128, :], xsk[s][ti]

    def load_x(l, ti, xt):
        ap, dk = src_ap(l, ti)
        DMA('sp', xt[:], ap, R=[dk] if dk else [], W=[xt], tk=xt)

    def transpose_mod(xt, hT, w, on_dve=False):
        for half in range(2):
            b = ps_ring('A')
            for j in range(4):
                kc = half * 4 + j
                TR(PS[:, b, j * 128:(j + 1) * 128], xt[:, kc * 128:(kc + 1) * 128], ident[:], R=[xt, ident], W=[PSk[b]])
            for j in range(4):
                kc = half * 4 + j
                if on_dve:
                    TS('dve', hT[:, kc, :], PS[:, b, j * 128:(j + 1) * 128], modc[:, 8 + kc, w:w + 1], modc[:, kc, w:w + 1],
                       ALU.mult, ALU.add, R=[PSk[b], modc], W=[hT])
                else:
                    ACT(hT[:, kc, :], PS[:, b, j * 128:(j + 1) * 128], AF.Identity, R=[PSk[b], modc], W=[hT],
                        scale=modc[:, 8 + kc, w:w + 1], bias=modc[:, kc, w:w + 1])

    ep_ctr = [0]
    stats = sb([128, 2, 6], F32, "stats")
    mv = sb([128, 2], F32, "mv")
    rstd = sb([128, 1], F32, "rstd")

    def epilogue(l, ti, ybanks, xt, w):
        dap, dk = dst_ap(l, ti)
        if dap is None:
            return
        r = ep_ring[ep_ctr[0] % 2]
        ep_ctr[0] += 1
        yv = PS[:, ybanks[0]:ybanks[0] + 2, :].rearrange("p a b -> p (a b)")
        TT('dve', r[:], yv, gate_rep[w][:], ALU.mult, R=[PSk[ybanks[0]], PSk[ybanks[1]], gate_rep[w]], W=[r])
        STT(r[:], xt[:], ALPHA, r[:], ALU.mult, ALU.add, R=[xt, r], W=[r])
        P.op('dve', lambda e: e.bn_stats(out=stats[:, 0, :], in_=r[:, 0:512]), ks([r]), ks([stats]))
        P.op('dve', lambda e: e.bn_stats(out=stats[:, 1, :], in_=r[:, 512:1024]), ks([r, stats]), ks([stats]))
        P.op('dve', lambda e: e.bn_aggr(out=mv[:], in_=stats[:].rearrange("p a b -> p (a b)")), ks([stats]), ks([mv]))
        TS('dve', rstd[:], mv[:, 1:2], eps_ln[:, 0:1], None, ALU.add, ALU.bypass, R=[mv, eps_ln], W=[rstd])
        TT('pool', rstd[:], rstd[:], mhalf[:, 0:1], ALU.pow, R=[rstd, mhalf], W=[rstd])
        TS('dve', r[:], r[:], mv[:, 0:1], rstd[:, 0:1], ALU.subtract, ALU.mult, R=[r, mv, rstd], W=[r])
        TT('dve' if l % 2 == 1 else 'pool', r[:], r[:], lng_rep[:], ALU.mult, R=[r, lng_rep], W=[r])
        TT('pool', r[:], r[:], lnb_rep[:], ALU.add, R=[r, lnb_rep], W=[r])
        tok = DMA('sp', dap, r[:], R=[r], W=[dk] if dk else [], tk=r)
        if l == layers[-1]:
            out_stores.append(r)

    def ab_layer(l):
        j = l // 2
        layer_setup(l)
        NXT = 3
        xt_ring = [sb([128, D], F32, "xt%d" % i) for i in range(NXT)]
        hT_ring = [sb([128, 8, 128], BF16, "hT%d" % i) for i in range(2)]
        qf_ring = [sb([128, 4, 128], BF16, "qf%d" % i) for i in range(2)]
        sga_ring = [sb([128, 4, 128], F32, "sga%d" % i) for i in range(2)]
        U_ring = [sb([128, 4, 130], F32, "U%d" % i) for i in range(3)]
        bg_ring = [sb([128, 4, 128], F32, "bg%d" % i) for i in range(2)]
        sgb_ring = [sb([128, 4, 128], F32, "sgb%d" % i) for i in range(2)]
        cs_ring = [sb([128, 2, 128], F32, "cs%d" % i) for i in range(2)]
        def kTs(pr, kt):
            return WB[pr, kt // 9, AB_W + (kt % 9) * 128:AB_W + (kt % 9 + 1) * 128]
        kTk = [Tk("kT%d" % i) for i in range(NTT)]
        Vx = sb([128, NTT, 256], BF16, "Vx")
        Vxk = [Tk("Vx%d" % i) for i in range(NTT)]
        t1 = sb([128, 4, 128], F32, "t1")
        t2 = sb([128, 4, 128], F32, "t2")
        xb_sb = sb([128, 4, 128], F32, "xb_sb")
        PT_ring = [sb([128, 512], BF16, "PT%d" % i) for i in range(3)]
        pt_ctr = [0]
        rec = sb([128, 512], F32, "rec")
        ot = sb([128, 512], F32, "ot")
        aT = sb([128, 4, 128], BF16, "aT")
        bT = sb([128, 4, 128], BF16, "bT")
        cy = sb([128, 4, 128], F32, "cy")
        maskL = sb([128, 4, 128], BF16, "maskL")
        maskR = sb([128, 4, 128], BF16, "maskR")
        cw = sb([128, 4, 3], F32, "cw")
        sk = sb([1, 8], F32, "sk")
        esrow = sb([1, 8, 128], BF16, "esrow")
        sel = sb([1, 2, 128], BF16, "sel")
        CP('pool', maskL[:], ones_b[:, 0:512].rearrange("p (a b) -> p a b", a=4), R=[ones_b], W=[maskL])
        CP('pool', maskR[:], ones_b[:, 0:512].rearrange("p (a b) -> p a b", a=4), R=[ones_b], W=[maskR])
        P.op('pool', lambda e: e.affine_select(out=maskL[:], in_=maskL[:], pattern=[[0, 4], [-1, 128]], compare_op=ALU.is_ge,
                                               fill=0.0, base=0, channel_multiplier=1), ks([maskL]), ks([maskL]))
        P.op('pool', lambda e: e.affine_select(out=maskR[:], in_=maskR[:], pattern=[[0, 4], [1, 128]], compare_op=ALU.is_ge,
                                               fill=0.0, base=0, channel_multiplier=-1), ks([maskR]), ks([maskR]))
        DMA('sp', cw[:], convc[j], W=[cw], tk=cw)
        DMA('sp', sk[:], sink_ab[j], W=[sk], tk=sk)
        ACT(sk[:], sk[:], AF.Exp, R=[sk], W=[sk])
        CP('dve', esrow[:], sk[0:1, :].unsqueeze(2).to_broadcast([1, 8, 128]), R=[sk], W=[esrow])
        MEMSET('pool', sel[:], 0.0, W=[sel])
        MEMSET('pool', sel[0:1, 0, 64:128], 1.0, W=[sel])
        MEMSET('pool', sel[0:1, 1, 0:64], 1.0, W=[sel])
        if dbg:
            print("AB sbuf remaining", nc.sbuf_bytes_remaining)
        MEMSET('pool', Vx[:], 1.0, W=Vxk)
        for u in U_ring:
            MEMSET('pool', u[:], 0.0, W=[u])

        def bankv(b):
            return PS[:, b, :].rearrange("p (a t) -> p a t", a=4)

        def proj_fm(hT, bank, chunks):
            for s_, cc in enumerate(chunks):
                for kc in range(8):
                    MM(PS[:, bank, s_ * 128:(s_ + 1) * 128], WB[:, kc, cc * 128:(cc + 1) * 128], hT[:, kc, :],
                       start=(kc == 0), stop=(kc == 7), R=[WBk[kc], hT], W=[PSk[bank]])

        def stage1a(ti):
            is_ctx = ti < 2
            w = 1 if is_ctx else 0
            xt = xt_ring[ti % NXT]
            hT = hT_ring[ti % 2]
            U = U_ring[ti % 3]
            cs = cs_ring[ti % 2]
            load_x(l, ti, xt)
            if not is_ctx:
                t0 = (ti - 2) * 128
                DMA('sp', cs[:, 0, :], cos_in[:, t0:t0 + 128], W=[cs], tk=cs)
                DMA('sp', cs[:, 1, :], sin_in[:, t0:t0 + 128], R=[cs], W=[cs], tk=cs)
            transpose_mod(xt, hT, w, on_dve=True)
            yield
            bk = ps_ring('B')
            proj_fm(hT, bk, [8] if is_ctx else [8, 9])
            for kc in range(8):
                MM(PS[:, bk, 256:384], hT[:, kc, :], WB[:, kc, 10 * 128:11 * 128], start=(kc == 0), stop=(kc == 7),
                   R=[WBk[kc], hT], W=[PSk[bk]])
            kslice = kTs(slice(0, 128), ti)
            if is_ctx:
                CP('act', kslice, PS[:, bk, 0:128], R=[PSk[bk]], W=[kTk[ti]])
            else:
                TT('dve', t1[:, 0, :], PS[:, bk, 0:128], cs[:, 0, :], ALU.mult, R=[PSk[bk], cs], W=[t1])
                TT('dve', t2[:, 0, :], PS[:, bk, 128:256], cs[:, 1, :], ALU.mult, R=[PSk[bk], cs], W=[t2])
                TT('pool', kslice, t1[:, 0, :], t2[:, 0, :], ALU.add, R=[t1, t2], W=[kTk[ti]])
            CP('act', Vx[:, ti, :].rearrange("p (a b) -> p a b", b=64)[:, 0::3, :],
               PS[:, bk, 256:384].rearrange("p (a b) -> p a b", b=64), R=[PSk[bk]], W=[Vxk[ti]])
            yield
            bxb = ps_ring('B')
            proj_fm(hT, bxb, [15, 16, 17, 18])
            CP('dve', xb_sb[:], bankv(bxb), R=[PSk[bxb]], W=[xb_sb])
            bcg = ps_ring('B')
            proj_fm(hT, bcg, [23, 24, 25, 26])
            TT('dve', U[:, :, 1:129], bankv(bcg), xb_sb[:], ALU.mult, R=[PSk[bcg], xb_sb], W=[U])
            first_of_seq = ti in (0, 2)
            Up = U_ring[(ti - 1) % 3]
            if first_of_seq:
                MEMSET('pool', U[:, :, 0:1], 0.0, W=[U])
                if ti == 2:
                    MEMSET('pool', Up[:, :, 129:130], 0.0, W=[Up])
            else:
                CP('pool', U[:, :, 0:1], Up[:, :, 128:129], R=[Up], W=[U])
                CP('pool', Up[:, :, 129:130], U[:, :, 1:2], R=[U], W=[Up])
            if ti == NTT - 1:
                MEMSET('pool', U[:, :, 129:130], 0.0, W=[U])
            yield

        def stage1b(ti):
            is_ctx = ti < 2
            hT = hT_ring[ti % 2]
            qf = qf_ring[ti % 2]
            sga = sga_ring[ti % 2]
            bg = bg_ring[ti % 2]
            sgb = sgb_ring[ti % 2]
            cs = cs_ring[ti % 2]
            bq = ps_ring('B')
            proj_fm(hT, bq, [0, 1, 2, 3])
            if is_ctx:
                CP('act', qf[:], bankv(bq), R=[PSk[bq]], W=[qf])
            else:
                yield
                br = ps_ring('B')
                proj_fm(hT, br, [4, 5, 6, 7])
                TT('dve', t1[:], bankv(bq), cs[:, 0:1, :].to_broadcast([128, 4, 128]), ALU.mult, R=[PSk[bq], cs], W=[t1])
                TT('dve', t2[:], bankv(br), cs[:, 1:2, :].to_broadcast([128, 4, 128]), ALU.mult, R=[PSk[br], cs], W=[t2])
                TT('pool', qf[:], t1[:], t2[:], ALU.add, R=[t1, t2], W=[qf])
            yield
            def silu_from_bank(dst, bank):
                ACT(dst[:], bankv(bank), AF.Exp, R=[PSk[bank]], W=[dst], scale=-1.0)
                ACT(dst[:], dst[:], AF.Ln, R=[dst], W=[dst], bias=one_col[:, 0:1])
                ACT(dst[:], dst[:], AF.Exp, R=[dst], W=[dst], scale=-1.0)
                TT('dve', dst[:], dst[:], bankv(bank), ALU.mult, R=[dst, PSk[bank]], W=[dst])

            bga = ps_ring('B')
            proj_fm(hT, bga, [11, 12, 13, 14])
            silu_from_bank(sga, bga)
            yield
            bbg = ps_ring('B')
            proj_fm(hT, bbg, [19, 20, 21, 22])
            CP('dve', bg[:], bankv(bbg), R=[PSk[bbg]], W=[bg])
            yield
            bgb = ps_ring('B')
            proj_fm(hT, bgb, [27, 28, 29, 30])
            silu_from_bank(sgb, bgb)
            yield

        def stage2(ti):
            is_ctx = ti < 2
            w = 1 if is_ctx else 0
            xt = xt_ring[ti % NXT]
            qf = qf_ring[ti % 2]
            sga = sga_ring[ti % 2]
            U = U_ring[ti % 3]
            bg = bg_ring[ti % 2]
            sgb = sgb_ring[ti % 2]
            chunks = [(0, None), (1, None)]
            if not is_ctx:
                if ti - 1 >= 2:
                    chunks.append((ti - 1, maskL))
                chunks.append((ti, None))
                if ti + 1 < NTT:
                    chunks.append((ti + 1, maskR))
            items = [(kt, mask, kvh) for (kt, mask) in chunks for kvh in range(2)]

            def score(it):
                kt, mask, kvh = it
                pr = slice(64 * kvh, 64 * kvh + 64)
                sbk = ps_ring('C')
                MM(PS[:, sbk, :], kTs(pr, kt), qf[pr, :, :], R=[kTk[kt], qf], W=[PSk[sbk]])
                return sbk

            def conv_fc(fc):
                TS('dve', cy[:, fc, :], U[:, fc, 0:128], cw[:, fc, 0:1], None, ALU.mult, ALU.bypass, R=[U, cw], W=[cy])
                STT(cy[:, fc, :], U[:, fc, 1:129], cw[:, fc, 1:2], cy[:, fc, :], ALU.mult, ALU.add, R=[U, cw, cy], W=[cy])
                STT(cy[:, fc, :], U[:, fc, 2:130], cw[:, fc, 2:3], cy[:, fc, :], ALU.mult, ALU.add, R=[U, cw, cy], W=[cy])

            assert len(items) >= 4
            pending = score(items[0])
            seen = set()
            for n, it in enumerate(items):
                kt, mask, kvh = it
                sbk = pending
                if n + 1 < len(items):
                    pending = score(items[n + 1])
                acc = 6 + kvh
                pt = PT_ring[pt_ctr[0] % 3]
                pt_ctr[0] += 1
                ACT(pt[:], PS[:, sbk, :], AF.Exp, R=[PSk[sbk]], W=[pt], scale=0.125)
                if mask is not None:
                    TT('pool', pt[:], pt[:], mask[:].rearrange("p a b -> p (a b)"), ALU.mult, R=[pt, mask], W=[pt])
                MM(PS[:, acc, :], Vx[:, kt, kvh * 128:(kvh + 1) * 128], pt[:], start=(kvh not in seen), stop=False,
                   R=[Vxk[kt], pt], W=[PSk[acc]])
                seen.add(kvh)
                if n < 4:
                    conv_fc(n)
                if n == 3:
                    TT('pool', cy[:], cy[:], bg[:], ALU.mult, R=[cy, bg], W=[cy])
                    TT('pool', bT[:], cy[:], sgb[:], ALU.mult, R=[cy, sgb], W=[bT])
                yield
            for kvh in range(2):
                acc = 6 + kvh
                MM(PS[:, acc, :], sel[0:1, kvh, :], esrow[0:1, kvh * 4:(kvh + 1) * 4, :].rearrange("p a b -> p (a b)"),
                   start=False, stop=True, R=[sel, esrow], W=[PSk[acc]])
            for kvh in range(2):
                acc = 6 + kvh
                pr = slice(64 * kvh, 64 * kvh + 64)
                dn = slice(64 * (1 - kvh), 64 * (1 - kvh) + 64)
                CP('dve', rec[pr, :], PS[dn, acc, :], R=[PSk[acc]], W=[rec])
                ACT(rec[pr, :], rec[pr, :], AF.Ln, R=[rec], W=[rec])
                ACT(rec[pr, :], rec[pr, :], AF.Exp, R=[rec], W=[rec], scale=-1.0)
                TT('dve', ot[pr, :], PS[pr, acc, :], rec[pr, :], ALU.mult, R=[PSk[acc], rec], W=[ot])
                TT('pool', aT[pr, :, :], ot[pr, :].rearrange("p (a b) -> p a b", a=4), sga[pr, :, :], ALU.mult,
                   R=[ot, sga], W=[aT])
                yield
            for nh in range(2):
                for fcc in range(8):
                    lhs = aT[:, fcc, :] if fcc < 4 else bT[:, fcc - 4, :]
                    MM(PS[:, nh, :], lhs, WOB[:, fcc, nh * 512:(nh + 1) * 512], start=(fcc == 0), stop=(fcc == 7),
                       R=[aT if fcc < 4 else bT, WOBk[fcc]], W=[PSk[nh]])
            epilogue(l, ti, (0, 1), xt, w)

        def chain(*gs):
            for g in gs:
                if g is not None:
                    yield from g

        run_gen(stage1a(0))
        run_gen(stage1b(0))
        run_gen(stage1a(1))
        for ti in range(1, NTT):
            interleave(chain(stage1b(ti), stage1a(ti + 1) if ti + 1 < NTT else None), stage2(ti - 1))
        run_gen(stage2(NTT - 1))

    def c_layer(l):
        j = l // 2
        ctx_out = l < DEPTH - 1
        layer_setup(l)
        NCH, CS, HC = 4, 32, 16
        xt_ring = [sb([128, D], F32, "cxt%d" % i) for i in range(2)]
        hT = sb([128, 8, 128], BF16, "chT")
        qs = sb([128, 8, 128], F32, "qs")
        sA = sb([128, 8, 128], F32, "sA")
        kk = sb([128, 8, 128], F32, "kk")
        sA2 = sb([128, 8, 128], F32, "sA2")
        kk2 = sb([128, 8, 128], F32, "kk2")
        sA_d = [sA, sA2]
        kk_d = [kk, kk2]
        E1 = sb([128, 8, 128], F32, "E1")
        E2 = sb([128, 8, 128], F32, "E2")
        Qp_r = [sb([128, 8, 128], BF16, "Qp%d" % i) for i in range(2)]
        Kp_r = [sb([128, 8, 128], BF16, "Kp%d" % i) for i in range(2)]
        Qpb = sb([128, 8, 128], BF16, "Qpb")
        Kpb = sb([128, 8, 128], BF16, "Kpb")
        Kz = sb([128, 8, 128], BF16, "Kz")
        vsb_r = [sb([128, D], BF16, "vsb%d" % i) for i in range(2)]
        sg = ep_ring[1]
        of = ep_ring[0]
        Ktok = sb([128, 8, 128], BF16, "Ktok")
        AT = sb([128, 8, CS], BF16, "AT")
        S_h = [sb([128, 4, 128], F32, "S%d" % i) for i in range(2)]
        SA_h = [sb([128, 4, 128], F32, "SA%d" % i) for i in range(2)]
        SAb_h = [sb([128, 4, 128], BF16, "SAb%d" % i) for i in range(2)]
        abF_r = [sb([128, 40], F32, "abF%d" % i) for i in range(2)]
        abB = sb([128, NTT, 40], F32, "abB")
        alf_d = [sb([128, 32], F32, "alf%d" % i) for i in range(2)]
        abB_k = [Tk("abB%d" % i) for i in range(NTT)]
        bet = sb([128, 32], F32, "bet")
        lbt = sb([128, 2, 8, 2], F32, "lbt")
        lb = sb([128, 2, 8], F32, "lb")
        oml = sb([128, 2, 8], F32, "oml")
        noml = sb([128, 2, 8], F32, "noml")
        gn_rep = sb([128, 128], F32, "gn_rep")
        maskF = sb([128, 8, CS], BF16, "maskF")
        maskB = sb([128, 8, CS], BF16, "maskB")
        hmF = sb([128, 128], BF16, "hmF")
        hmB = sb([128, 128], BF16, "hmB")
        rmF = sb([128, 8, 128], BF16, "rmF")
        rmB = sb([128, 8, 128], BF16, "rmB")
        ss = sb([128, 8], F32, "ss")
        mcol = sb([128, 32], F32, "mcol")
        gT = sb([128, 8, 128], BF16, "gT")
        tmpo = E1
        for d_ in range(2):
            DMA('sp', lbt[:, d_, :, :], lbcols[d_], R=[lbt], W=[lbt], tk=lbt)
        if j == 0:
            MEMSET('pool', lb[:], 0.0, W=[lb])
        else:
            TT('dve', lb[:], lbt[:, :, :, 1], lbt[:, :, :, 0], ALU.subtract, R=[lbt], W=[lb])
            ACT(lb[:], lb[:], AF.Sigmoid, R=[lb], W=[lb])
        TS('dve', oml[:], lb[:], -1.0, 1.0, ALU.mult, ALU.add, R=[lb], W=[oml])
        TS('dve', noml[:], oml[:], -1.0, None, ALU.mult, ALU.bypass, R=[oml], W=[noml])
        DMA('sp', gn_rep[:], gnorm_c[j].partition_broadcast(128), W=[gn_rep], tk=gn_rep)
        for m_, sign in ((maskF, 1), (maskB, -1)):
            CP('pool', m_[:], ones_b[:, 0:8 * CS].rearrange("p (a b) -> p a b", a=8), R=[ones_b], W=[m_])
            for qd in range(NCH):
                pr = slice(CS * qd, CS * qd + CS)
                P.op('pool', lambda e, m_=m_, pr=pr, sign=sign: e.affine_select(
                    out=m_[pr, :, :], in_=m_[pr, :, :], pattern=[[0, 8], [sign, CS]], compare_op=ALU.is_ge,
                    fill=0.0, base=0, channel_multiplier=-sign), ks([m_]), ks([m_]))

        def chv(ap):
            return ap.rearrange("p h (c t) -> p (h c) t", t=CS)

        MEMSET('pool', rmF[:], 1.0, W=[rmF])
        MEMSET('pool', rmB[:], 1.0, W=[rmB])
        MEMSET('pool', chv(rmF[:])[:, :, 0:1], 0.0, W=[rmF])
        MEMSET('pool', chv(rmB[:])[:, :, CS - 1:CS], 0.0, W=[rmB])
        MEMSET('pool', hmF[:], 0.0, W=[hmF])
        MEMSET('pool', hmB[:], 0.0, W=[hmB])
        MEMSET('pool', hmF[:].rearrange("p (c t) -> p c t", t=CS)[:, :, 0:HC], 1.0, W=[hmF])
        MEMSET('pool', hmB[:].rearrange("p (c t) -> p c t", t=CS)[:, :, HC:CS], 1.0, W=[hmB])

        def flat(b_):
            return b_[:].rearrange("p h t -> p (h t)")

        def zsig(d_, zbanks):
            for hb in range(2):
                ACT(sA_d[d_][:, hb * 4:hb * 4 + 4, :], PS[:, zbanks[hb], :].rearrange("p (a t) -> p a t", a=4), AF.Sigmoid,
                    R=[PSk[zbanks[hb]]], W=[sA_d[d_]])

        def prep_dir(ti, d_, Qo, Ko, abv, abk):
            sX, kX, alf = sA_d[d_], kk_d[d_], alf_d[d_]
            if j == 0:
                TS('pool', kX[:], sX[:], -1.0, 1.0, ALU.mult, ALU.add, R=[sX], W=[kX])
                yield
                ACT(sX[:], sX[:], AF.Ln, R=[sX], W=[sX])
            else:
                TT('dve', sX[:], sX[:], oml[:, d_, :].unsqueeze(2).to_broadcast([128, 8, 128]), ALU.mult, R=[sX, oml], W=[sX])
                TT('dve', sX[:], sX[:], lb[:, d_, :].unsqueeze(2).to_broadcast([128, 8, 128]), ALU.add, R=[sX, lb], W=[sX])
                yield
                TS('pool', kX[:], sX[:], -1.0, 1.0, ALU.mult, ALU.add, R=[sX], W=[kX])
                ACT(sX[:], sX[:], AF.Ln, R=[sX], W=[sX])
            yield
            if d_ == 0:
                P.op('dve', lambda e: e.tensor_tensor_scan(out=flat(sX), data0=flat(rmF), data1=flat(sX), initial=0.0,
                                                           op0=ALU.mult, op1=ALU.add), ks([sX, rmF]), ks([sX]))
                mid, last = HC - 1, CS - 1
            else:
                P.op('dve', lambda e: e.tensor_tensor_scan(out=flat(sX)[:, ::-1], data0=flat(rmB)[:, ::-1],
                                                           data1=flat(sX)[:, ::-1], initial=0.0,
                                                           op0=ALU.mult, op1=ALU.add), ks([sX, rmB]), ks([sX]))
                mid, last = HC, 0
            yield
            CP('dve', mcol[:], chv(sX[:])[:, :, mid], R=[sX], W=[mcol])
            ACT(alf[:], mcol[:], AF.Exp, R=[mcol], W=[alf])
            TT('dve', chv(sX[:]), chv(sX[:]), mcol[:].unsqueeze(2).to_broadcast([128, 32, CS]), ALU.subtract,
               R=[sX, mcol], W=[sX])
            yield
            ACT(E1[:], sX[:], AF.Exp, R=[sX], W=[E1])
            ACT(E2[:], sX[:], AF.Exp, R=[sX], W=[E2], scale=-1.0)
            CP('pool', bet[:], chv(E1[:])[:, :, last], R=[E1], W=[bet])
            al3 = alf[:].rearrange("p (h c) -> p h c", c=NCH)
            ga3 = abv[:, 8:40].rearrange("p (h c) -> p h c", c=NCH)
            be3 = bet[:].rearrange("p (h c) -> p h c", c=NCH)
            if d_ == 0:
                TT('dve', ga3[:, :, 0:NCH - 1], be3[:, :, 0:NCH - 1], al3[:, :, 1:NCH], ALU.mult, R=[bet, alf], W=[abk])
                CP('dve', ga3[:, :, NCH - 1:NCH], be3[:, :, NCH - 1:NCH], R=[bet], W=[abk])
                CP('dve', abv[:, 0:8], al3[:, :, 0], R=[alf], W=[abk])
            else:
                TT('dve', ga3[:, :, 1:NCH], be3[:, :, 1:NCH], al3[:, :, 0:NCH - 1], ALU.mult, R=[bet, alf], W=[abk])
                CP('dve', ga3[:, :, 0:1], be3[:, :, 0:1], R=[bet], W=[abk])
                CP('dve', abv[:, 0:8], al3[:, :, NCH - 1], R=[alf], W=[abk])
            STT(Qo[:], qs[:], 128.0 ** -0.5, E1[:], ALU.mult, ALU.mult, R=[qs, E1], W=[Qo])
            TT('pool', Ko[:], kX[:], E2[:], ALU.mult, R=[kX, E2], W=[Ko])
            yield

        def izip(*gens):
            gens = [g for g in gens if g is not None]
            while gens:
                for g in list(gens):
                    try:
                        next(g)
                        yield
                    except StopIteration:
                        gens.remove(g)

        def scan_tile(ti, d_, Qo, Ko, vsb, want_out, abv, abk):
            afirst = abv[:, 0:8]
            ga3 = abv[:, 8:40].rearrange("p (h c) -> p h c", c=NCH)
            bt = ps_ring('A')
            ptb = PS[:, bt, :].bitcast(BF16).rearrange("p (h t) -> p h t", h=8)
            for h in range(8):
                TR(ptb[:, h, :], Ko[:, h, :], identb[:], R=[Ko, identb], W=[PSk[bt]])
            CP('act', Ktok[:], ptb, R=[PSk[bt]], W=[Ktok])
            yield
            if want_out:
                TT('pool', Kz[:], Ko[:], (hmF if d_ == 0 else hmB)[:].unsqueeze(1).to_broadcast([128, 8, 128]), ALU.mult,
                   R=[Ko, hmF, hmB], W=[Kz])
                ba = ps_ring('A')
                pa = PS[:, ba, 0:8 * CS].rearrange("p (h c) -> p h c", h=8)
                for cj in range(NCH):
                    pr = slice(CS * cj, CS * cj + CS)
                    c0 = CS * cj
                    for h in range(8):
                        if d_ == 0:
                            MM(pa[pr, h, 0:HC], Kz[:, h, pr], Qo[:, h, c0:c0 + HC], R=[Kz, Qo], W=[PSk[ba]], tp=(0, c0))
                            MM(pa[pr, h, HC:CS], Ko[:, h, pr], Qo[:, h, c0 + HC:c0 + CS], R=[Ko, Qo], W=[PSk[ba]], tp=(0, c0))
                        else:
                            MM(pa[pr, h, HC:CS], Kz[:, h, pr], Qo[:, h, c0 + HC:c0 + CS], R=[Kz, Qo], W=[PSk[ba]], tp=(0, c0))
                            MM(pa[pr, h, 0:HC], Ko[:, h, pr], Qo[:, h, c0:c0 + HC], R=[Ko, Qo], W=[PSk[ba]], tp=(0, c0))
                TT('dve', AT[:], pa, (maskF if d_ == 0 else maskB)[:], ALU.mult, R=[PSk[ba], maskF, maskB], W=[AT])
            order = list(range(NCH)) if d_ == 0 else list(range(NCH - 1, -1, -1))
            for n_, cj in enumerate(order):
                pr = slice(CS * cj, CS * cj + CS)
                if n_ == 0:
                    for hf in range(2):
                        TT('pool', SA_h[hf][:], S_h[hf][:], afirst[:, 4 * hf:4 * hf + 4].unsqueeze(2).to_broadcast([128, 4, 128]),
                           ALU.mult, R=[S_h[hf], abk], W=[SA_h[hf]])
                if want_out:
                    for hf in range(2):
                        CP('act', SAb_h[hf][:], SA_h[hf][:], R=[SA_h[hf]], W=[SAb_h[hf]])
                for hf in range(2):
                    ob = 6 + hf
                    if want_out:
                        for hh in range(4):
                            h = 4 * hf + hh
                            oo = PS[pr, ob, hh * 128:(hh + 1) * 128]
                            MM(oo, Qo[:, h, pr], SAb_h[hf][:, hh, :], start=(hh == 0), stop=False, R=[Qo, SAb_h[hf]], W=[PSk[ob]],
                               tp=(0, CS * cj), sgc=True)
                        for hh in range(4):
                            h = 4 * hf + hh
                            oo = PS[pr, ob, hh * 128:(hh + 1) * 128]
                            MM(oo, AT[pr, h, :], vsb[pr, h * 128:(h + 1) * 128], start=False, stop=True, R=[AT, vsb], W=[PSk[ob]],
                               tp=(CS * cj, CS * cj), sgc=True)
                    pb_ = 4 + hf
                    for hh in range(4):
                        h = 4 * hf + hh
                        MM(PS[:, pb_, hh * 128:(hh + 1) * 128], Ktok[pr, h, :], vsb[pr, h * 128:(h + 1) * 128],
                           R=[Ktok, vsb], W=[PSk[pb_]], tp=(CS * cj, 0))
                yield
                for hf in range(2):
                    TT('dve', SA_h[hf][:], SA_h[hf][:], PS[:, 4 + hf, :].rearrange("p (h t) -> p h t", h=4), ALU.add,
                       R=[SA_h[hf], PSk[4 + hf]], W=[SA_h[hf]])
                for hf in range(2):
                    dst = S_h[hf] if n_ == NCH - 1 else SA_h[hf]
                    TT('pool', dst[:], SA_h[hf][:], ga3[:, 4 * hf:4 * hf + 4, cj:cj + 1].to_broadcast([128, 4, 128]), ALU.mult,
                       R=[SA_h[hf], abk], W=[dst])
                yield

        for hf in range(2):
            MEMSET('pool', S_h[hf][:], 0.0, W=[S_h[hf]])

        def front(ti):
            is_ctx = ti < 2
            w = 1 if is_ctx else 0
            xt = xt_ring[ti % 2]
            vsb = vsb_r[ti % 2]
            load_x(l, ti, xt)
            transpose_mod(xt, hT, w, on_dve=True)
            yield

            def proj_fm(bank, c0):
                for s_ in range(4):
                    cc = c0 + s_
                    for kc in range(8):
                        MM(PS[:, bank, s_ * 128:(s_ + 1) * 128], WB[:, kc, cc * 128:(cc + 1) * 128], hT[:, kc, :],
                           start=(kc == 0), stop=(kc == 7), R=[WBk[kc], hT], W=[PSk[bank]])

            def proj_tm(bank, c0):
                for kc in range(8):
                    MM(PS[:, bank, :], hT[:, kc, :], WB[:, kc, c0 * 128:(c0 + 4) * 128], start=(kc == 0), stop=(kc == 7),
                       R=[WBk[kc], hT], W=[PSk[bank]])

            for hb in range(2):
                b = ps_ring('B')
                proj_fm(b, hb * 4)
                ACT(qs[:, hb * 4:hb * 4 + 4, :], PS[:, b, :].rearrange("p (a t) -> p a t", a=4), AF.Silu, R=[PSk[b]], W=[qs])
                yield
            for hb in range(2):
                b = ps_ring('B')
                proj_tm(b, 24 + hb * 4)
                CP('act', vsb[:, hb * 512:(hb + 1) * 512], PS[:, b, :], R=[PSk[b]], W=[vsb])
                yield
            for hb in range(2):
                b = ps_ring('B')
                proj_tm(b, 32 + hb * 4)
                ACT(sg[:, hb * 512:(hb + 1) * 512], PS[:, b, :], AF.Silu, R=[PSk[b]], W=[sg])
                yield
            def zproj(d_):
                zb = []
                for hb in range(2):
                    b = ps_ring('B')
                    proj_fm(b, 8 + d_ * 8 + hb * 4)
                    zb.append(b)
                    yield
                zsig(d_, zb)
                yield

            def chain2(*gs):
                for g in gs:
                    yield from g

            yield from zproj(0)
            yield from izip(prep_dir(ti, 0, Qp_r[ti % 2], Kp_r[ti % 2], abF_r[ti % 2][:, :], abF_r[ti % 2].k),
                            chain2(zproj(1), prep_dir(ti, 1, Qpb, Kpb, abB[:, ti, :], abB_k[ti])))
            want = ctx_out or not is_ctx
            if want:
                DMA('sp', sc_sg[ti], sg[:], R=[sg], W=[sck[ti]], tk=sg)
                DMA('sp', sc_q[ti], flat(Qpb), R=[Qpb], W=[sck[ti]], tk=Qpb)
            DMA('sp', sc_k[ti], flat(Kpb), R=[Kpb], W=[sck[ti]], tk=Kpb)
            DMA('sp', sc_v[ti], vsb[:], R=[vsb], W=[sck[ti]], tk=vsb)

        def scan1(ti):
            is_ctx = ti < 2
            want = ctx_out or not is_ctx
            yield from scan_tile(ti, 0, Qp_r[ti % 2], Kp_r[ti % 2], vsb_r[ti % 2], want, abF_r[ti % 2][:, :], abF_r[ti % 2].k)
            if want:
                CP('act', of[:], PS[:, 6:8, :].rearrange("p a b -> p (a b)"), R=[PSk[6], PSk[7]], W=[of])
                DMA('sp', sc_of[ti], of[:], R=[of], W=[sck[ti]], tk=of)

        run_gen(front(0))
        for ti in range(NTT):
            interleave(front(ti + 1) if ti + 1 < NTT else None, scan1(ti))

        for hf in range(2):
            MEMSET('pool', S_h[hf][:], 0.0, W=[S_h[hf]])
        order = [1, 0] + list(range(NTT - 1, 1, -1))

        def f2(b_):
            return b_[:] if len(b_[:].shape) == 2 else b_[:].rearrange("p h t -> p (h t)")

        Kpb_s = [Kpb, Kp_r[0]]
        Qpb_s = [Qpb, Qp_r[0]]
        of_s = [sA2, qs]
        sg_s = [kk2, sA]
        tmpo_s = [E1, kk]

        def wants(ti):
            return ctx_out or ti >= 2

        def p2_load(n_):
            ti = order[n_]
            sl = n_ % 2
            DMA('sp', f2(Kpb_s[sl]), sc_k[ti], R=[sck[ti]], W=[Kpb_s[sl]], tk=Kpb_s[sl])
            DMA('sp', vsb_r[sl][:], sc_v[ti], R=[sck[ti]], W=[vsb_r[sl]], tk=vsb_r[sl])
            yield
            if wants(ti):
                DMA('sp', f2(Qpb_s[sl]), sc_q[ti], R=[sck[ti]], W=[Qpb_s[sl]], tk=Qpb_s[sl])
                yield

        def p2_load_b(n_):
            ti = order[n_]
            sl = n_ % 2
            if wants(ti):
                load_x(l, ti, xt_ring[sl])
                DMA('sp', f2(of_s[sl]), sc_of[ti], R=[sck[ti]], W=[of_s[sl]], tk=of_s[sl])
                DMA('sp', f2(sg_s[sl]), sc_sg[ti], R=[sck[ti]], W=[sg_s[sl]], tk=sg_s[sl])
            yield

        def p2_scan(n_):
            ti = order[n_]
            sl = n_ % 2
            yield from scan_tile(ti, 1, Qpb_s[sl], Kpb_s[sl], vsb_r[sl], wants(ti), abB[:, ti, :], abB_k[ti])
            if wants(ti):
                TT('dve', f2(tmpo_s[sl]), PS[:, 6:8, :].rearrange("p a b -> p (a b)"), f2(of_s[sl]), ALU.add,
                   R=[PSk[6], PSk[7], of_s[sl]], W=[tmpo_s[sl]])
            yield

        def p2_read(n_):
            ti = order[n_]
            sl = n_ % 2
            if not wants(ti):
                return
            w = 1 if ti < 2 else 0
            tmpo, of_, sg_ = tmpo_s[sl], of_s[sl], sg_s[sl]
            ACT(f2(of_), f2(tmpo), AF.Square, R=[tmpo], W=[of_])
            P.op('dve', lambda e: e.tensor_reduce(out=ss[:], in_=f2(of_).rearrange("p (h t) -> p h t", h=8), axis=AX.X, op=ALU.add),
                 ks([of_]), ks([ss]))
            TS('dve', ss[:], ss[:], 1.0 / 128.0, RMS_EPS, ALU.mult, ALU.add, R=[ss], W=[ss])
            TT('pool', ss[:], ss[:], mhalf[:], ALU.pow, R=[ss, mhalf], W=[ss])
            yield
            t3 = f2(tmpo).rearrange("p (h t) -> p h t", h=8)
            TT('dve', t3, t3, ss[:].unsqueeze(2).to_broadcast([128, 8, 128]), ALU.mult, R=[tmpo, ss], W=[tmpo])
            TT('dve', t3, t3, gn_rep[:].unsqueeze(1).to_broadcast([128, 8, 128]), ALU.mult, R=[tmpo, gn_rep], W=[tmpo])
            yield
            TT('dve', f2(of_), f2(tmpo), f2(sg_), ALU.mult, R=[tmpo, sg_], W=[of_])
            yield
            for half in range(2):
                b = ps_ring('A')
                for s_ in range(4):
                    h = half * 4 + s_
                    TR(PS[:, b, s_ * 128:(s_ + 1) * 128], f2(of_)[:, h * 128:(h + 1) * 128], ident[:], R=[of_, ident], W=[PSk[b]])
                CP('act', gT[:, half * 4:half * 4 + 4, :], PS[:, b, :].rearrange("p (a t) -> p a t", a=4), R=[PSk[b]], W=[gT])
                yield
            for nh in range(2):
                for h in range(8):
                    MM(PS[:, 2 + nh, :], gT[:, h, :], WOB[:, h, nh * 512:(nh + 1) * 512], start=(h == 0), stop=(h == 7),
                       R=[gT, WOBk[h]], W=[PSk[2 + nh]])
                yield
            epilogue(l, ti, (2, 3), xt_ring[sl], w)

        NO = len(order)
        run_gen(p2_load(0))
        run_gen(p2_load_b(0))
        for n_ in range(NO + 1):
            interleave(p2_scan(n_) if n_ < NO else None,
                       p2_read(n_ - 1) if n_ >= 1 else None,
                       p2_load(n_ + 1) if n_ + 1 < NO else None)
            if n_ + 1 < NO:
                run_gen(p2_load_b(n_ + 1))

    for l in layers:
        with ExitStack() as es:
            cur_stack[0] = es
            if l % 2 == 0:
                ab_layer(l)
            else:
                c_layer(l)
            barrier()
            cur_stack[0] = None
    fin = []
    seen = set()
    for r in out_stores:
        if id(r) not in seen:
            seen.add(id(r))
            fin.append(r)
    P.op('sp', lambda e: e.nop(), (), ks(fin))
    P.emit()
    return nc


def _rope_tables(L):
    half = 16
    freqs = 10000.0 ** (-np.arange(half, dtype=np.float32) / half)
    t = np.arange(L)
    row = (t // 64).astype(np.float32)
    col = (t % 64).astype(np.float32)
    cos_t = np.zeros((128, L), np.float32)
    sin_t = np.zeros((128, L), np.float32)
    for p in range(128):
        d = p % 64
        if d < 32:
            ang = row * freqs[d % 16]
            sgn = -1.0 if d < 16 else 1.0
        else:
            ang = col * freqs[(d - 32) % 16]
            sgn = -1.0 if (d - 32) < 16 else 1.0
        cos_t[p] = np.cos(ang.astype(np.float32))
        sin_t[p] = sgn * np.sin(ang.astype(np.float32))
    return cos_t, sin_t


def _ab_col_perm():
    rot = np.zeros(64, np.int64)
    for d in range(64):
        base = 0 if d < 32 else 32
        dd = d - base
        rot[d] = base + (dd + 16 if dd < 16 else dd - 16)
    qcols, qrot = [], []
    for jj in range(4):
        for h in (jj, 4 + jj):
            qcols += [h * 64 + d for d in range(64)]
            qrot += [h * 64 + int(rot[d]) for d in range(64)]
    kcols = [512 + i for i in range(128)]
    krot = [512 + kv * 64 + int(rot[d]) for kv in range(2) for d in range(64)]
    vcols = [640 + i for i in range(128)]
    gacols = [768 + c for c in qcols]
    rest = list(range(1280, 3328))
    cols = qcols + qrot + kcols + krot + vcols + gacols + rest
    assert len(cols) == AB_W
    rows_out = qcols + list(range(512, 1024))
    return np.array(cols), np.array(rows_out)


def make_in_maps(inputs, L=4096, nb=8):
    f = lambda a: np.ascontiguousarray(np.asarray(a, dtype=np.float32))
    x = f(inputs["x"]); c = f(inputs["c"]); ctx = f(inputs["ctx"]); c_ctx = f(inputs["c_ctx"])
    cols, rows_out = _ab_col_perm()
    w_in_ab = f(f(inputs["w_in_ab"])[:, :, cols])
    w_out_ab = f(f(inputs["w_out_ab"])[:, rows_out, :])
    conv = f(inputs["conv_ab"])
    convc = f(conv.reshape(2, 3, 4, 128).transpose(0, 3, 2, 1))
    lbc = f(inputs["lb_c"])
    lbcols = f(lbc.reshape(2, 2, 8, 128).transpose(0, 3, 2, 1))
    cos_t, sin_t = _rope_tables(L)
    shared = dict(
        w_ada=f(inputs["w_ada"]), bcols=f(f(inputs["b_ada"]).reshape(DEPTH, 24, 128).transpose(0, 2, 1)),
        ln_g=f(inputs["ln_g"]).reshape(DEPTH, 1, D), ln_b=f(inputs["ln_b"]).reshape(DEPTH, 1, D),
        w_in_ab=w_in_ab, w_out_ab=w_out_ab, sink_ab=f(inputs["sink_ab"]).reshape(2, 1, 8), convc=convc,
        w_in_c=f(inputs["w_in_c"]), w_out_c=f(inputs["w_out_c"]), lbcols=lbcols,
        gnorm_c=f(inputs["gnorm_c"]).reshape(2, 1, 128), cos_t=cos_t, sin_t=sin_t)
    maps = []
    for b in range(nb):
        cc = np.stack([c[b], c_ctx], -1).reshape(8, 128, 2).transpose(1, 0, 2)
        m = dict(shared)
        m.update(x=f(x[b]), ctx=f(ctx[b]), ccols=f(cc))
        maps.append(m)
    return maps


_CACHE = {}


def kernel(**inputs):
    L = int(np.asarray(inputs["x"]).shape[1])
    nb = int(np.asarray(inputs["x"]).shape[0])
    if L not in _CACHE:
        _CACHE[L] = build_program(L)
    nc = _CACHE[L]
    maps = make_in_maps(inputs, L, nb)
    res = run_bass_kernel_spmd(nc, maps, core_ids=list(range(nb)))
    return np.stack([np.asarray(r["out"]) for r in res.results], 0).astype(np.float32)
```
